# Optimizing a Trainium2 kernel written in Bass

```python
import math, functools
import jax, jax.numpy as jnp
from jax import lax
import numpy as np

D_MODEL = 1024
BATCH = 2
SEQ = 8192
DEPTH = 1
DEC_BATCH = 128
DEC_SEQ = 8
PAST_LEN = 2048
PAGE_SIZE = 128

D_RNN = D_MODEL
N_LRU_BLOCKS = 8
LRU_BLOCK = D_RNN // N_LRU_BLOCKS
LRU_CONV_W = 4
LRU_C = 8.0
N_HEADS = 8
HEAD_DIM = D_MODEL // (2 * N_HEADS)
V_DIM = 2 * HEAD_DIM
QK_WIDTH = N_HEADS * 2 * HEAD_DIM
V_WIDTH = N_HEADS * V_DIM
Q_BLOCK = 128
N_MEM = 256
MEM_HEADS = 4
MEM_HEAD_DIM = D_MODEL // MEM_HEADS
MEM_WIDTH = MEM_HEADS * MEM_HEAD_DIM
N_BRANCH = 3
D_FF = 3 * D_MODEL
FFN_CONV_W = 3
RMS_EPS = 1e-6
NEG_INF = -1e30
IN_SPLITS = (D_RNN, D_RNN, QK_WIDTH, QK_WIDTH, V_WIDTH, MEM_WIDTH, N_BRANCH * D_MODEL)
IN_WIDTH = sum(IN_SPLITS)

kernel_name = "hawk_diffattn_memxattn_convffn_step"


def rms_norm(x, g):
    xf = x.astype(jnp.float32)
    y = xf * lax.rsqrt(jnp.mean(xf * xf, axis=-1, keepdims=True) + RMS_EPS)
    return y.astype(x.dtype) * g


def causal_dwconv(x, prev, w, b):
    width = w.shape[0]
    t = x.shape[1]
    xp = jnp.concatenate([prev.astype(x.dtype), x], axis=1)
    y = sum(xp[:, j:j + t] * w[j] for j in range(width)) + b
    return y, xp[:, xp.shape[1] - (width - 1):]


def rg_lru(x, h0, w_rg_a, b_rg_a, w_rg_i, b_rg_i, lru_lambda):
    b, t, c = x.shape
    xb = x.reshape(b, t, N_LRU_BLOCKS, LRU_BLOCK)
    r = jax.nn.sigmoid((jnp.einsum('btnc,ncd->btnd', xb, w_rg_a).reshape(b, t, c) + b_rg_a).astype(jnp.float32))
    i = jax.nn.sigmoid((jnp.einsum('btnc,ncd->btnd', xb, w_rg_i).reshape(b, t, c) + b_rg_i).astype(jnp.float32))
    log_a = -LRU_C * jax.nn.softplus(-lru_lambda.astype(jnp.float32)) * r
    a = jnp.exp(log_a)
    u = jnp.sqrt(-jnp.expm1(2.0 * log_a)) * (i * x.astype(jnp.float32))

    def step(h, au):
        a_t, u_t = au
        h = a_t * h + u_t
        return h, h

    h_last, hs = lax.scan(step, h0.astype(jnp.float32), (a.transpose(1, 0, 2), u.transpose(1, 0, 2)))
    return hs.transpose(1, 0, 2).astype(x.dtype), h_last


def diff_attn_prompt(q, k, v, lam):
    b, t = q.shape[:2]
    nb = t // Q_BLOCK
    scale = 1.0 / math.sqrt(HEAD_DIM)
    kpos = jnp.arange(t)
    qb = q.reshape(b, nb, Q_BLOCK, N_HEADS, 2, HEAD_DIM).transpose(1, 0, 2, 3, 4, 5)

    def block(args):
        qi, idx = args
        s = jnp.einsum('bqhcd,bkhcd->bhcqk', qi, k).astype(jnp.float32) * scale
        qpos = idx * Q_BLOCK + jnp.arange(Q_BLOCK)
        s = jnp.where(kpos[None, :] <= qpos[:, None], s, NEG_INF)
        p = jax.nn.softmax(s, axis=-1)
        a = p[:, :, 0] - lam * p[:, :, 1]
        return jnp.einsum('bhqk,bkhe->bqhe', a.astype(v.dtype), v)

    o = lax.map(block, (qb, jnp.arange(nb)))
    return o.transpose(1, 0, 2, 3, 4).reshape(b, t, N_HEADS, V_DIM)


def diff_attn_sample(q, k, v, lam, cache_k, cache_v, page_table):
    bd, t = q.shape[:2]
    scale = 1.0 / math.sqrt(HEAD_DIM)
    kp = cache_k[page_table].reshape(bd, -1, N_HEADS, 2, HEAD_DIM).astype(q.dtype)
    vp = cache_v[page_table].reshape(bd, -1, N_HEADS, V_DIM).astype(v.dtype)
    n_past = kp.shape[1]
    s_past = jnp.einsum('bqhcd,bkhcd->bhcqk', q, kp).astype(jnp.float32) * scale
    s_new = jnp.einsum('bqhcd,bkhcd->bhcqk', q, k).astype(jnp.float32) * scale
    causal = jnp.tril(jnp.ones((t, t), dtype=bool))
    s_new = jnp.where(causal, s_new, NEG_INF)
    p = jax.nn.softmax(jnp.concatenate([s_past, s_new], axis=-1), axis=-1)
    a = (p[:, :, 0] - lam * p[:, :, 1]).astype(v.dtype)
    return (jnp.einsum('bhqk,bkhe->bqhe', a[..., :n_past], vp)
            + jnp.einsum('bhqk,bkhe->bqhe', a[..., n_past:], v))


def memory_kv(mem, g_mem, w_mem_kv):
    b, n, _ = mem.shape
    kv = rms_norm(mem, g_mem) @ w_mem_kv
    mk, mv = jnp.split(kv, 2, axis=-1)
    return mk.reshape(b, n, MEM_HEADS, MEM_HEAD_DIM), mv.reshape(b, n, MEM_HEADS, MEM_HEAD_DIM)


def memory_attn(cq, mk, mv):
    b, t, _ = cq.shape
    q = cq.reshape(b, t, MEM_HEADS, MEM_HEAD_DIM)
    s = jnp.einsum('bqhd,bkhd->bhqk', q, mk.astype(q.dtype)).astype(jnp.float32) / math.sqrt(MEM_HEAD_DIM)
    p = jax.nn.softmax(s, axis=-1).astype(cq.dtype)
    return jnp.einsum('bhqk,bkhd->bqhd', p, mv.astype(cq.dtype)).reshape(b, t, MEM_WIDTH)


def lambda_init_fn(layer_idx):
    return 0.8 - 0.6 * math.exp(-0.3 * layer_idx)


def layer(x, attn_fn, mem_k, mem_v, lru_h0, lru_conv0, ffn_conv0, lam, lam_init,
          g_pre_mix, w_in, w_lru_conv, b_lru_conv, w_rg_a, b_rg_a, w_rg_i, b_rg_i, lru_lambda,
          g_subln, w_br_lru, w_br_attn, w_br_mem, w_out, g_post_mix,
          g_pre_ffn, w_up, w_ffn_conv, b_ffn_conv, w_down, g_post_ffn):
    b, t, _ = x.shape
    h = rms_norm(x, g_pre_mix)
    z = h @ w_in
    offsets = np.cumsum(IN_SPLITS)[:-1].tolist()
    lru_x, lru_y, q, k, v, cq, gates = jnp.split(z, offsets, axis=-1)
    xc, lru_conv_new = causal_dwconv(lru_x, lru_conv0, w_lru_conv, b_lru_conv)
    hs, lru_h_new = rg_lru(xc, lru_h0, w_rg_a, b_rg_a, w_rg_i, b_rg_i, lru_lambda)
    br_lru = jax.nn.gelu(lru_y) * hs
    q = q.reshape(b, t, N_HEADS, 2, HEAD_DIM)
    k = k.reshape(b, t, N_HEADS, 2, HEAD_DIM)
    v = v.reshape(b, t, N_HEADS, V_DIM)
    o = attn_fn(q, k, v, lam)
    br_attn = (rms_norm(o, g_subln) * (1.0 - lam_init)).reshape(b, t, V_WIDTH)
    br_mem = memory_attn(cq, mem_k, mem_v)
    g = jax.nn.sigmoid(gates.astype(jnp.float32)).astype(x.dtype).reshape(b, t, N_BRANCH, D_MODEL)
    m = (g[:, :, 0] * (br_lru @ w_br_lru) + g[:, :, 1] * (br_attn @ w_br_attn)
         + g[:, :, 2] * (br_mem @ w_br_mem))
    x = x + rms_norm(m @ w_out, g_post_mix)
    up, ffn_conv_new = causal_dwconv(rms_norm(x, g_pre_ffn) @ w_up, ffn_conv0, w_ffn_conv, b_ffn_conv)
    gate, val = jnp.split(up, 2, axis=-1)
    x = x + rms_norm((jax.nn.gelu(gate) * val) @ w_down, g_post_ffn)
    return x, k, v, lru_h_new, lru_conv_new, ffn_conv_new


def setup_inputs(seed: int = 0) -> dict:
    key = jax.random.key(seed)
    ks = iter(jax.random.split(key, 64))
    f32 = jnp.float32

    def nrm(shape, s):
        return jax.random.normal(next(ks), shape, f32) * s

    def gain(shape):
        return 1.0 + nrm(shape, 0.05)

    n_pages = PAST_LEN // PAGE_SIZE
    n_used = DEC_BATCH * n_pages
    n_pool = (5 * n_used + 3) // 4
    page_table = jax.random.permutation(next(ks), n_pool)[:n_used].reshape(DEC_BATCH, n_pages).astype(jnp.int32)
    a0 = jax.random.uniform(next(ks), (DEPTH, D_RNN), f32, 0.9, 0.999)
    return {
        "x_prompt": nrm((BATCH, SEQ, D_MODEL), 1.0),
        "x_sample": nrm((DEC_BATCH, DEC_SEQ, D_MODEL), 1.0),
        "mem_prompt": nrm((BATCH, N_MEM, D_MODEL), 1.0),
        "cache_k": nrm((DEPTH, n_pool, PAGE_SIZE, N_HEADS, 2, HEAD_DIM), 1.0),
        "cache_v": nrm((DEPTH, n_pool, PAGE_SIZE, N_HEADS, V_DIM), 1.0),
        "page_table": page_table,
        "cache_mem_k": nrm((DEPTH, DEC_BATCH, N_MEM, MEM_HEADS, MEM_HEAD_DIM), 1.0),
        "cache_mem_v": nrm((DEPTH, DEC_BATCH, N_MEM, MEM_HEADS, MEM_HEAD_DIM), 1.0),
        "state_lru_h": nrm((DEPTH, DEC_BATCH, D_RNN), 0.5),
        "state_lru_conv": nrm((DEPTH, DEC_BATCH, LRU_CONV_W - 1, D_RNN), 1.0),
        "state_ffn_conv": nrm((DEPTH, DEC_BATCH, FFN_CONV_W - 1, 2 * D_FF), 1.0),
        "g_pre_mix": gain((DEPTH, D_MODEL)),
        "w_in": nrm((DEPTH, D_MODEL, IN_WIDTH), D_MODEL ** -0.5),
        "w_lru_conv": nrm((DEPTH, LRU_CONV_W, D_RNN), LRU_CONV_W ** -0.5),
        "b_lru_conv": nrm((DEPTH, D_RNN), 0.01),
        "w_rg_a": nrm((DEPTH, N_LRU_BLOCKS, LRU_BLOCK, LRU_BLOCK), LRU_BLOCK ** -0.5),
        "b_rg_a": nrm((DEPTH, D_RNN), 0.01),
        "w_rg_i": nrm((DEPTH, N_LRU_BLOCKS, LRU_BLOCK, LRU_BLOCK), LRU_BLOCK ** -0.5),
        "b_rg_i": nrm((DEPTH, D_RNN), 0.01),
        "lru_lambda": jnp.log(a0) - jnp.log1p(-a0),
        "lambda_q1": nrm((DEPTH, HEAD_DIM), 0.1),
        "lambda_k1": nrm((DEPTH, HEAD_DIM), 0.1),
        "lambda_q2": nrm((DEPTH, HEAD_DIM), 0.1),
        "lambda_k2": nrm((DEPTH, HEAD_DIM), 0.1),
        "g_subln": gain((DEPTH, V_DIM)),
        "g_mem": gain((DEPTH, D_MODEL)),
        "w_mem_kv": nrm((DEPTH, D_MODEL, 2 * MEM_WIDTH), D_MODEL ** -0.5),
        "w_br_lru": nrm((DEPTH, D_RNN, D_MODEL), D_RNN ** -0.5),
        "w_br_attn": nrm((DEPTH, V_WIDTH, D_MODEL), V_WIDTH ** -0.5),
        "w_br_mem": nrm((DEPTH, MEM_WIDTH, D_MODEL), MEM_WIDTH ** -0.5),
        "w_out": nrm((DEPTH, D_MODEL, D_MODEL), D_MODEL ** -0.5),
        "g_post_mix": gain((DEPTH, D_MODEL)),
        "g_pre_ffn": gain((DEPTH, D_MODEL)),
        "w_up": nrm((DEPTH, D_MODEL, 2 * D_FF), D_MODEL ** -0.5),
        "w_ffn_conv": nrm((DEPTH, FFN_CONV_W, 2 * D_FF), FFN_CONV_W ** -0.5),
        "b_ffn_conv": nrm((DEPTH, 2 * D_FF), 0.01),
        "w_down": nrm((DEPTH, D_FF, D_MODEL), D_FF ** -0.5),
        "g_post_ffn": gain((DEPTH, D_MODEL)),
    }


def reference(x_prompt, x_sample, mem_prompt, cache_k, cache_v, page_table, cache_mem_k, cache_mem_v,
              state_lru_h, state_lru_conv, state_ffn_conv,
              g_pre_mix, w_in, w_lru_conv, b_lru_conv, w_rg_a, b_rg_a, w_rg_i, b_rg_i, lru_lambda,
              lambda_q1, lambda_k1, lambda_q2, lambda_k2, g_subln, g_mem, w_mem_kv,
              w_br_lru, w_br_attn, w_br_mem, w_out, g_post_mix,
              g_pre_ffn, w_up, w_ffn_conv, b_ffn_conv, w_down, g_post_ffn):
    xp, xs = x_prompt, x_sample
    bp = xp.shape[0]
    kp_l, vp_l, mkp_l, mvp_l, hp_l, cp_l, fp_l = [], [], [], [], [], [], []
    ks_l, vs_l, hs_l, cs_l, fs_l = [], [], [], [], []
    for l in range(DEPTH):
        lam_init = lambda_init_fn(l)
        lam = (jnp.exp(jnp.sum(lambda_q1[l] * lambda_k1[l]).astype(jnp.float32))
               - jnp.exp(jnp.sum(lambda_q2[l] * lambda_k2[l]).astype(jnp.float32)) + lam_init)
        lw = [a[l] for a in (g_pre_mix, w_in, w_lru_conv, b_lru_conv, w_rg_a, b_rg_a, w_rg_i, b_rg_i,
                             lru_lambda, g_subln, w_br_lru, w_br_attn, w_br_mem, w_out, g_post_mix,
                             g_pre_ffn, w_up, w_ffn_conv, b_ffn_conv, w_down, g_post_ffn)]
        mk_p, mv_p = memory_kv(mem_prompt, g_mem[l], w_mem_kv[l])
        xp, k_p, v_p, h_p, c_p, f_p = layer(
            xp, diff_attn_prompt, mk_p, mv_p,
            jnp.zeros((bp, D_RNN), jnp.float32),
            jnp.zeros((bp, LRU_CONV_W - 1, D_RNN), xp.dtype),
            jnp.zeros((bp, FFN_CONV_W - 1, 2 * D_FF), xp.dtype),
            lam, lam_init, *lw)
        attn_s = functools.partial(diff_attn_sample, cache_k=cache_k[l], cache_v=cache_v[l], page_table=page_table)
        xs, k_s, v_s, h_s, c_s, f_s = layer(
            xs, attn_s, cache_mem_k[l], cache_mem_v[l],
            state_lru_h[l], state_lru_conv[l], state_ffn_conv[l],
            lam, lam_init, *lw)
        kp_l.append(k_p); vp_l.append(v_p); mkp_l.append(mk_p); mvp_l.append(mv_p)
        hp_l.append(h_p); cp_l.append(c_p); fp_l.append(f_p)
        ks_l.append(k_s); vs_l.append(v_s); hs_l.append(h_s); cs_l.append(c_s); fs_l.append(f_s)
    return (xp, xs,
            jnp.stack(kp_l), jnp.stack(vp_l), jnp.stack(mkp_l), jnp.stack(mvp_l),
            jnp.stack(hp_l), jnp.stack(cp_l), jnp.stack(fp_l),
            jnp.stack(ks_l), jnp.stack(vs_l), jnp.stack(hs_l), jnp.stack(cs_l), jnp.stack(fs_l))
```

```python
import contextlib
import numpy as np
import concourse.bass as bass
import concourse.mybir as mybir
from concourse.bass_utils import run_bass_kernel_spmd

F32 = mybir.dt.float32
BF16 = mybir.dt.bfloat16
I32 = mybir.dt.int32
AF = mybir.ActivationFunctionType
ALU = mybir.AluOpType
AX = mybir.AxisListType


class _Op:
    __slots__ = ("eng", "fn", "deps", "is_dma", "signal", "semkey", "sem", "val", "idx")


class Sched:
    INORDER = {"pe"}

    def __init__(self, nc, stack, n_dma_sems=80):
        self.nc = nc
        self.ops = []
        self.last_w = {}
        self.readers = {}
        self.engs = {"pe": nc.tensor, "act": nc.scalar, "dve": nc.vector, "pool": nc.gpsimd, "sp": nc.sync}
        self.eng_sems = {e: [stack.enter_context(nc.semaphore(f"c_{e}"))] for e in self.engs}
        self.free_dma_sems = [stack.enter_context(nc.semaphore(f"d{i}")) for i in range(n_dma_sems)]
        self.dma_sem = {}
        self.muted = False

    def _mk(self, eng, fn, r, w, is_dma, semkey=None):
        if self.muted:
            return None
        w = tuple(w) + tuple(k for k in r if str(k).startswith("ps"))
        r = tuple(k for k in r if not str(k).startswith("ps"))
        op = _Op()
        op.eng, op.fn, op.is_dma, op.signal, op.semkey = eng, fn, is_dma, bool(is_dma), semkey
        op.sem = None
        op.val = 0
        op.idx = len(self.ops)
        deps = {}
        for k in r:
            d = self.last_w.get(k)
            if d is not None:
                deps[d.idx] = d
        for k in w:
            d = self.last_w.get(k)
            if d is not None:
                deps[d.idx] = d
            for d in self.readers.get(k, ()):
                deps[d.idx] = d
        need = []
        for d in deps.values():
            if d is op:
                continue
            if d.is_dma or is_dma or d.eng != eng or eng not in self.INORDER:
                d.signal = True
                need.append(d)
        op.deps = need
        for k in w:
            self.last_w[k] = op
            self.readers[k] = []
        for k in r:
            if k in w:
                continue
            self.readers.setdefault(k, []).append(op)
        self.ops.append(op)
        return op

    def add(self, eng, fn, r=(), w=()):
        return self._mk(eng, fn, tuple(r), tuple(w), False)

    def dma(self, q, out, in_, r=(), w=(), semkey=None, **kw):
        if semkey is None:
            semkey = w[0] if (w and not str(w[0]).startswith("D:")) else r[0]

        def fn(e, out=out, in_=in_, kw=kw):
            return e.dma_start(out=out, in_=in_, **kw)
        return self._mk(q, fn, tuple(r), tuple(w), True, semkey)

    def idma(self, out, in_, idx_ap, r=(), w=()):
        def fn(e):
            return e.indirect_dma_start(out=out, out_offset=None, in_=in_,
                                        in_offset=bass.IndirectOffsetOnAxis(ap=idx_ap, axis=0))
        return self._mk("pool", fn, tuple(r), tuple(w), True, w[0])

    def emit(self):
        nc = self.nc
        cnt = {e: 0 for e in self.engs}
        dcnt = {}
        for op in self.ops:
            if not op.signal:
                continue
            if op.is_dma:
                if op.semkey not in self.dma_sem:
                    self.dma_sem[op.semkey] = self.free_dma_sems.pop()
                op.sem = self.dma_sem[op.semkey]
                dcnt[op.semkey] = dcnt.get(op.semkey, 0) + 16
                op.val = dcnt[op.semkey]
            else:
                cnt[op.eng] += 1
                if cnt[op.eng] > 30000:
                    self.eng_sems[op.eng].append(self.free_dma_sems.pop())
                    cnt[op.eng] = 1
                op.sem = self.eng_sems[op.eng][-1]
                op.val = cnt[op.eng]
        per_eng = {e: [op for op in self.ops if op.eng == e] for e in self.engs}

        def run(ename, engine):
            waited = {}
            for op in per_eng[ename]:
                for d in op.deps:
                    key = id(d.sem)
                    if waited.get(key, 0) < d.val:
                        engine.wait_ge(d.sem, d.val)
                        waited[key] = d.val
                if op.fn is None:
                    continue
                ins = op.fn(engine)
                if op.signal:
                    ins.then_inc(op.sem, 16 if op.is_dma else 1)

        with nc.Block() as block:
            @block.tensor
            def _(e):
                run("pe", e)

            @block.scalar
            def _(e):
                run("act", e)

            @block.vector
            def _(e):
                run("dve", e)

            @block.gpsimd
            def _(e):
                run("pool", e)

            @block.sync
            def _(e):
                run("sp", e)


NPRE = 47
TS = 3
NCOL = TS * 128
LIGHT_SGS = [(t, min(t + TS, NPRE)) for t in range(0, NPRE, TS)]
FULL_SGS = [(47, 50), (50, 53), (53, 56), (56, 59), (59, 62), (62, 64)]
NPOOL = 2560
GK = 1.5957691216057308


class _Stop(Exception):
    pass


def build(light_sgs=LIGHT_SGS, full_sgs=FULL_SGS, dbg=99):
    nc = bass.Bass("TRN2", target_bir_lowering=False)
    din = lambda n, s, d=F32: nc.dram_tensor(n, list(s), d, kind="ExternalInput").ap()
    dout = lambda n, s, d=F32: nc.dram_tensor(n, list(s), d, kind="ExternalOutput").ap()
    xs = din("xs", [8320, 1024]); memx = din("memx", [256, 1024])
    ck = din("ck", [NPOOL * 128, 1024]); cv = din("cv", [NPOOL * 128, 1024])
    pt = din("pt", [1, 256], I32)
    cmk = din("cmk", [4096, 1024]); cmv = din("cmv", [4096, 1024])
    st_h = din("st_h", [16, 1024]); st_c = din("st_c", [48, 1024]); st_f = din("st_f", [32, 6144])
    w_in = din("w_in", [1024, 9216]); w_rga = din("w_rga", [1024, 128]); w_rgi = din("w_rgi", [1024, 128])
    w_mkv = din("w_mkv", [1024, 2048])
    w_br = [din(f"w_br{i}", [1024, 1024]) for i in range(3)]
    w_out = din("w_out", [1024, 1024]); w_up = din("w_up", [1024, 6144]); w_down = din("w_down", [3072, 1024])
    vec8_d = din("vec8", [128, 11, 8]); ffnv_d = din("ffnv", [128, 4, 48]); bc_d = din("bc", [128, 2, 1024])
    lamv_d = din("lamv", [128, 4, 64]); gsub_d = din("gsub", [128, 1]); consts_d = din("consts", [128, 3, 128])
    iota_d = din("iota", [128, 1]); masks_d = din("masks", [128, 2, 64])

    y_own = dout("y_own", [2048, 1024]); y_s = dout("y_s", [128, 1024])
    k_own = dout("k_own", [2048, 1024]); v_own = dout("v_own", [2048, 1024])
    k_s = dout("k_s", [128, 1024]); v_s = dout("v_s", [128, 1024])
    mem_k = dout("mem_k", [256, 1024]); mem_v = dout("mem_v", [256, 1024])
    lru_h_p = dout("lru_h_p", [1, 1024]); lru_conv_p = dout("lru_conv_p", [3, 1024]); ffn_conv_p = dout("ffn_conv_p", [2, 6144])
    lru_h_s = dout("lru_h_s", [16, 1024]); lru_conv_s = dout("lru_conv_s", [16, 3, 1024]); ffn_conv_s = dout("ffn_conv_s", [16, 2, 6144])
    KTs = nc.dram_tensor("KTs", [8, 128, 8192], BF16, kind="Internal").ap()
    Vs = nc.dram_tensor("Vs", [8, 8192, 128], BF16, kind="Internal").ap()
    OUTKEYS = []

    with contextlib.ExitStack() as st:
        S = Sched(nc, st)
        st.enter_context(nc.allow_non_contiguous_dma("small strided state/layout DMAs"))
        sb = lambda n, s, d=F32: st.enter_context(nc.sbuf_tensor("s_" + n, list(s), d))
        ps = [st.enter_context(nc.psum_tensor(f"ps{i}", [128, 512], F32)) for i in range(7)]
        pstb_t = st.enter_context(nc.psum_tensor("ps7", [128, 1024], BF16))
        pstb = pstb_t[:]

        consts = sb("consts", [128, 3, 128]); identb = sb("identb", [128, 128], BF16)
        trib = sb("trib", [128, 128], BF16); smaskb = sb("smaskb", [128, 128], BF16)
        onesb = sb("onesb", [128, 128], BF16); onesf = sb("onesf", [128, 128]); zerob = sb("zerob", [128, 128], BF16)
        vec8 = sb("vec8", [128, 11, 8]); ffnv = sb("ffnv", [128, 4, 48]); bcg = sb("bcg", [128, 2, 1024])
        lamv = sb("lamv", [128, 4, 64]); gsub = sb("gsub", [128, 1]); iota = sb("iota", [128, 1])
        masks = sb("masks", [128, 2, 64]); eps = sb("eps", [128, 1]); small = sb("small", [128, 16])
        coef = sb("coef", [128, 2, 8]); lamt = sb("lamt", [128, 64])
        pti = sb("pti", [128, 256], I32); ptf = sb("ptf", [128, 256]); pidx = sb("pidx", [128, 256], I32)
        wa_b = sb("wa_b", [128, 8, 128], BF16); wi_b = sb("wi_b", [128, 8, 128], BF16)
        xt = [sb(f"xt{i}", [128, 1024]) for i in range(2)]
        junk = sb("junk", [128, 1024], BF16)
        hb = [sb(f"hb{i}", [128, 1024], BF16) for i in range(2)]
        ssb = sb("ssb", [128, 4]);
        hT = sb("hT", [128, 8, NCOL], BF16); QT = sb("QT", [128, 8, NCOL], BF16)
        BR = [sb(f"BR{i}", [128, 8, NCOL], BF16) for i in range(3)]
        wf = [sb(f"wf{i}", [128, 8, 256]) for i in range(2)]
        wb = [sb(f"wb{i}", [128, 8, 256], BF16) for i in range(3)]
        T = [sb(f"T{i}", [128, NCOL]) for i in range(8)]
        EXTL = sb("EXTL", [128, 3 + NCOL]); EXTS = sb("EXTS", [128, 16, 11])
        EXTG = sb("EXTG", [128, 2 + NCOL]); EXTV = sb("EXTV", [128, 2 + NCOL])
        EXSG = sb("EXSG", [128, 16, 10]); EXSV = sb("EXSV", [128, 16, 10])
        KTh = sb("KTh", [128, 8192], BF16); Vh = sb("Vh", [128, 64, 128], BF16)
        Eb = [sb(f"Eb{i}", [128, 2, 128], BF16) for i in range(2)]
        KTp = sb("KTp", [128, 8, 128], BF16)
        MKT = sb("MKT", [128, 8, 256], BF16); MV = sb("MV", [128, 2, 1024], BF16)
        MKTs = sb("MKTs", [128, 8, 256], BF16); MVs = sb("MVs", [128, 2, 1024], BF16)
        ktb = sb("ktb", [128, NCOL], BF16); vbs = sb("vbs", [128, 1024], BF16)
        KTsm = sb("KTsm", [128, 8, 128], BF16)
        X1 = sb("X1", [128, TS, 1024]); YB = sb("YB", [128, TS, 1024])
        stT_c = sb("stT_c", [128, 8, 48]); stT_h = sb("stT_h", [128, 8, 16]); stT_f = sb("stT_f", [128, 48, 32])
        halo = sb("halo", [128, 8, 3]); hstate = sb("hstate", [128, 8]); FH = sb("FH", [128, 48, 2])
        hsS = sb("hsS", [128, 8, 16]); stg = sb("stg", [128, 256])

        bank = [0]

        def nb():
            b = bank[0] % 7
            bank[0] += 1
            return b

        def chk(n):
            if dbg == n:
                S.muted = True
        S.dma("sp", consts[:], consts_d, w=["consts"])
        S.dma("sp", vec8[:], vec8_d, w=["vec8"]); S.dma("sp", ffnv[:], ffnv_d, w=["ffnv"])
        S.dma("sp", bcg[:], bc_d, w=["bcg"]); S.dma("sp", lamv[:], lamv_d, w=["lamv"])
        S.dma("sp", gsub[:], gsub_d, w=["gsub"]); S.dma("sp", iota[:], iota_d, w=["iota"])
        S.dma("sp", masks[:], masks_d, w=["masks"])
        S.dma("sp", pti[:], pt.partition_broadcast(128), w=["pti"])
        chk(1)
        S.add("dve", lambda e: e.tensor_copy(out=identb[:], in_=consts[:, 0, :]), r=["consts"], w=["identb"])
        S.add("dve", lambda e: e.tensor_copy(out=trib[:], in_=consts[:, 1, :]), r=["consts"], w=["trib"])
        S.add("dve", lambda e: e.tensor_copy(out=smaskb[:], in_=consts[:, 2, :]), r=["consts"], w=["smaskb"])
        S.add("dve", lambda e: e.memset(onesb[:], 1.0), w=["onesb"])
        S.add("dve", lambda e: e.memset(onesf[:], 1.0), w=["onesf"])
        S.add("dve", lambda e: e.memset(zerob[:], 0.0), w=["zerob"])
        S.add("dve", lambda e: e.memset(eps[:], 1e-6), w=["eps"])
        S.add("dve", lambda e: e.memset(halo[:], 0.0), w=["halo"])
        S.add("dve", lambda e: e.memset(hstate[:], 0.0), w=["hstate"])
        S.add("dve", lambda e: e.memset(FH[:], 0.0), w=["FH"])
        S.add("act", lambda e: e.activation(out=coef[:, 0, :], in_=vec8[:, 8, :], func=AF.Exp, scale=-1.0), r=["vec8"], w=["coef"])
        S.add("dve", lambda e: e.tensor_scalar_add(out=coef[:, 0, :], in0=coef[:, 0, :], scalar1=1.0), r=["coef"], w=["coef"])
        S.add("act", lambda e: e.activation(out=coef[:, 0, :], in_=coef[:, 0, :], func=AF.Ln), r=["coef"], w=["coef"])
        S.add("dve", lambda e: e.tensor_scalar_mul(out=coef[:, 1, :], in0=coef[:, 0, :], scalar1=-16.0), r=["coef"], w=["coef"])
        S.add("dve", lambda e: e.tensor_scalar_mul(out=coef[:, 0, :], in0=coef[:, 0, :], scalar1=-8.0), r=["coef"], w=["coef"])
        for i in range(2):
            S.add("dve", lambda e, i=i: e.tensor_tensor(out=lamt[:], in0=lamv[:, 2 * i, :], in1=lamv[:, 2 * i + 1, :], op=ALU.mult), r=["lamv"], w=["lamt"])
            S.add("dve", lambda e, i=i: e.reduce_sum(out=small[:, 2 + i:3 + i], in_=lamt[:], axis=AX.X), r=["lamt"], w=["small"])
        S.add("act", lambda e: e.activation(out=small[:, 2:4], in_=small[:, 2:4], func=AF.Exp), r=["small"], w=["small"])
        S.add("dve", lambda e: e.tensor_tensor(out=small[:, 0:1], in0=small[:, 3:4], in1=small[:, 2:3], op=ALU.subtract), r=["small"], w=["small"])
        S.add("dve", lambda e: e.tensor_scalar_add(out=small[:, 0:1], in0=small[:, 0:1], scalar1=-0.2), r=["small"], w=["small"])
        S.add("dve", lambda e: e.tensor_scalar_mul(out=small[:, 1:2], in0=gsub[:], scalar1=0.8), r=["small", "gsub"], w=["small"])
        neglam = small[:, 0:1]; gsub08 = small[:, 1:2]
        S.add("dve", lambda e: e.tensor_copy(out=ptf[:], in_=pti[:]), r=["pti"], w=["ptf"])
        S.add("dve", lambda e: e.tensor_scalar(out=ptf[:], in0=ptf[:], scalar1=128.0, scalar2=iota[:, 0:1], op0=ALU.mult, op1=ALU.add), r=["ptf", "iota"], w=["ptf"])
        S.add("dve", lambda e: e.tensor_copy(out=pidx[:], in_=ptf[:]), r=["ptf"], w=["pidx"])
        chk(2)
        for (wd, wdst, nm) in ((w_rga, wa_b, "wa_b"), (w_rgi, wi_b, "wi_b")):
            S.dma("sp", wf[0][:, :, 0:128], wd.rearrange("(n c) d -> c n d", c=128), w=["wf0"])
            S.add("dve", lambda e, wdst=wdst: e.tensor_copy(out=wdst[:], in_=wf[0][:, :, 0:128]), r=["wf0"], w=[nm])

        chk(3)
        def tr_state(src, nrows, ncols, dst3, key):
            for c0 in range(0, ncols, 1024):
                S.dma("sp", xt[0][:nrows, :], src[:, c0:c0 + 1024], w=["xt0"])
                for k in range(8):
                    S.add("pe", lambda e, k=k: e.transpose(out=ps[6][:, k * nrows:(k + 1) * nrows], in_=xt[0][:nrows, k * 128:(k + 1) * 128], identity=consts[:nrows, 0, :nrows]),
                          r=["xt0", "consts"], w=["ps6"])
                S.add("dve", lambda e, c0=c0: e.tensor_copy(out=dst3[:, c0 // 128:c0 // 128 + 8, :], in_=ps[6][:, 0:8 * nrows].rearrange("p (a b) -> p a b", a=8)),
                      r=["ps6"], w=[key])
        tr_state(st_c, 48, 1024, stT_c, "stT_c")
        tr_state(st_h, 16, 1024, stT_h, "stT_h")
        tr_state(st_f, 32, 6144, stT_f, "stT_f")

        chk(4)
        xslot = [0]

        def front(src_rows, dstT, dkey, col0, xkeep=None):
            i = xslot[0] % 2
            xslot[0] += 1
            if xkeep is None:
                S.dma("sp", xt[i][:], src_rows, w=[f"xt{i}"])
                xin, xkey = xt[i][:], f"xt{i}"
            else:
                xin, xkey = xkeep
            S.add("act", lambda e: e.activation(out=junk[:], in_=xin, func=AF.Square, scale=1.0 / 32, accum_out=ssb[:, i:i + 1]), r=[xkey], w=["junk", f"ss{i}"])
            S.add("act", lambda e: e.activation(out=ssb[:, i:i + 1], in_=ssb[:, i:i + 1], func=AF.Sqrt, bias=eps[:, 0:1], scale=1.0), r=[f"ss{i}", "eps"], w=[f"ss{i}"])
            S.add("dve", lambda e: e.reciprocal(out=ssb[:, i:i + 1], in_=ssb[:, i:i + 1]), r=[f"ss{i}"], w=[f"ss{i}"])
            S.add("act", lambda e: e.activation(out=hb[i][:], in_=xin, func=AF.Copy, scale=ssb[:, i:i + 1]), r=[xkey, f"ss{i}"], w=[f"hb{i}"])
            for k in range(8):
                S.add("pe", lambda e, k=k: e.transpose(out=pstb[:, k * 128:(k + 1) * 128], in_=hb[i][:, k * 128:(k + 1) * 128], identity=identb[:]),
                      r=[f"hb{i}", "identb"], w=["ps7"])
            S.add("dve", lambda e: e.tensor_copy(out=dstT[:, :, col0:col0 + 128], in_=pstb.rearrange("p (a b) -> p a b", a=8)), r=["ps7"], w=[dkey])
            return i

        wslot = [0]

        def wslab(W, kc0, KC, f0, FW, gain=None):
            i = wslot[0] % 3
            fi_ = wslot[0] % 2
            wslot[0] += 1
            src = W[kc0 * 128:(kc0 + KC) * 128, f0:f0 + FW].rearrange("(kc p) f -> p kc f", p=128)
            S.dma("sp", wf[fi_][:, :KC, :FW], src, w=[f"wf{fi_}"])
            if gain is None:
                S.add("pool", lambda e: e.tensor_copy(out=wb[i][:, :KC, :FW], in_=wf[fi_][:, :KC, :FW]), r=[f"wf{fi_}"], w=[f"wb{i}"])
            else:
                S.add("dve", lambda e: e.tensor_tensor(out=wb[i][:, :KC, :FW], in0=wf[fi_][:, :KC, :FW],
                                                        in1=vec8[:, gain, kc0:kc0 + KC].unsqueeze(2).broadcast_to([128, KC, FW]), op=ALU.mult),
                      r=[f"wf{fi_}", "vec8"], w=[f"wb{i}"])
            return wb[i], f"wb{i}"

        def mm_fm(wbuf, wkey, KC, fi, actT, akey, c0, n, bank_=None, extra_r=()):
            b = nb() if bank_ is None else bank_
            for kc in range(KC):
                S.add("pe", lambda e, kc=kc: e.matmul(ps[b][:, :n], lhsT=wbuf[:, kc, fi * 128:(fi + 1) * 128], rhs=actT[:, kc, c0:c0 + n], start=(kc == 0), stop=(kc == KC - 1)),
                      r=[wkey, akey] + list(extra_r), w=[f"ps{b}"])
            return b

        def mm_tm(actT, akey, c0, wbuf, wkey, KC, FW, b, foff=0, start=True, stop=True):
            for kc in range(KC):
                S.add("pe", lambda e, kc=kc: e.matmul(ps[b][:, foff:foff + FW], lhsT=actT[:, kc, c0:c0 + 128], rhs=wbuf[:, kc, :FW], start=(start and kc == 0), stop=(stop and kc == KC - 1)),
                      r=[wkey, akey], w=[f"ps{b}"])

        def gelu_mul(dst, dkey, xin, xkey, other, okey, n, tA, tB):
            S.add("act", lambda e: e.activation(out=T[tA][:, :n], in_=xin, func=AF.Square), r=[xkey], w=[f"T{tA}"])
            S.add("dve", lambda e: e.tensor_scalar(out=T[tA][:, :n], in0=T[tA][:, :n], scalar1=0.044715, scalar2=1.0, op0=ALU.mult, op1=ALU.add), r=[f"T{tA}"], w=[f"T{tA}"])
            S.add("dve", lambda e: e.tensor_tensor(out=T[tA][:, :n], in0=T[tA][:, :n], in1=xin, op=ALU.mult), r=[f"T{tA}", xkey], w=[f"T{tA}"])
            S.add("act", lambda e: e.activation(out=T[tA][:, :n], in_=T[tA][:, :n], func=AF.Sigmoid, scale=GK), r=[f"T{tA}"], w=[f"T{tA}"])
            S.add("dve", lambda e: e.tensor_tensor(out=T[tA][:, :n], in0=T[tA][:, :n], in1=xin, op=ALU.mult), r=[f"T{tA}", xkey], w=[f"T{tA}"])
            S.add("dve", lambda e: e.tensor_tensor(out=dst, in0=T[tA][:, :n], in1=other, op=ALU.mult), r=[f"T{tA}", okey], w=[dkey])

        eslot = [0]

        def attn_block(qap0, qap1, qkey, kt0, kt1, kkey, vap, vkey, n, bias_ap, mask_ap, mkey, Ob, Zb, ocol, first, last, sb_):
            i = eslot[0] % 2
            eslot[0] += 1
            sb2_ = sb_ + 2
            S.add("pe", lambda e: e.matmul(ps[sb_][:, 0:n], lhsT=kt0, rhs=qap0, start=True, stop=True), r=[kkey, qkey], w=[f"ps{sb_}"])
            S.add("pe", lambda e: e.matmul(ps[sb2_][:, 0:n], lhsT=kt1, rhs=qap1, start=True, stop=True), r=[kkey, qkey], w=[f"ps{sb2_}"])
            E = Eb[i][:].rearrange("p a b -> p (a b)")[:, 0:2 * n]
            S.add("act", lambda e: e.activation(out=E[:, 0:n], in_=ps[sb_][:, 0:n], func=AF.Exp, scale=0.125, bias=bias_ap), r=[f"ps{sb_}", "masks"], w=[f"Eb{i}"])
            S.add("act", lambda e: e.activation(out=E[:, n:2 * n], in_=ps[sb2_][:, 0:n], func=AF.Exp, scale=0.125, bias=bias_ap), r=[f"ps{sb2_}", "masks"], w=[f"Eb{i}"])
            if mask_ap is not None:
                S.add("dve", lambda e: e.tensor_tensor(out=E.rearrange("p (a b) -> p a b", a=2), in0=E.rearrange("p (a b) -> p a b", a=2),
                                                        in1=mask_ap.unsqueeze(1).broadcast_to([128, 2, n]), op=ALU.mult), r=[f"Eb{i}", mkey], w=[f"Eb{i}"])
            S.add("pe", lambda e: e.matmul(ps[Ob][:, ocol:ocol + 2 * n], lhsT=vap, rhs=E, start=first, stop=last), r=[vkey, f"Eb{i}"], w=[f"ps{Ob}"])
            S.add("pe", lambda e: e.matmul(ps[Zb][:, ocol:ocol + 2 * n], lhsT=onesb[:], rhs=E, start=first, stop=last), r=["onesb", f"Eb{i}"], w=[f"ps{Zb}"])

        def attn_epilogue(Ob, Zb, ocol, n, dst, dkey):
            S.add("dve", lambda e: e.tensor_scalar_add(out=T[0][:, 0:2 * n], in0=ps[Zb][:, ocol:ocol + 2 * n], scalar1=1e-30), r=[f"ps{Zb}"], w=["T0"])
            S.add("dve", lambda e: e.reciprocal(out=T[0][:, 0:2 * n], in_=T[0][:, 0:2 * n]), r=["T0"], w=["T0"])
            S.add("dve", lambda e: e.tensor_tensor(out=T[0][:, 0:2 * n], in0=T[0][:, 0:2 * n], in1=ps[Ob][:, ocol:ocol + 2 * n], op=ALU.mult), r=["T0", f"ps{Ob}"], w=["T0"])
            S.add("dve", lambda e: e.scalar_tensor_tensor(out=T[1][:, 0:n], in0=T[0][:, n:2 * n], scalar=neglam, in1=T[0][:, 0:n], op0=ALU.mult, op1=ALU.add), r=["T0", "small"], w=["T1"])
            S.add("dve", lambda e: e.tensor_tensor(out=T[2][:, 0:n], in0=T[1][:, 0:n], in1=T[1][:, 0:n], op=ALU.mult), r=["T1"], w=["T2"])
            b = 6
            S.add("pe", lambda e: e.matmul(ps[b][:, 0:n], lhsT=onesf[:], rhs=T[2][:, 0:n], start=True, stop=True), r=["onesf", "T2"], w=[f"ps{b}"])
            S.add("act", lambda e: e.activation(out=T[2][:, 0:n], in_=ps[b][:, 0:n], func=AF.Sqrt, bias=eps[:, 0:1], scale=1.0 / 128), r=[f"ps{b}", "eps"], w=["T2"])
            S.add("dve", lambda e: e.reciprocal(out=T[2][:, 0:n], in_=T[2][:, 0:n]), r=["T2"], w=["T2"])
            S.add("dve", lambda e: e.scalar_tensor_tensor(out=dst, in0=T[1][:, 0:n], scalar=gsub08, in1=T[2][:, 0:n], op0=ALU.mult, op1=ALU.mult), r=["T1", "T2", "small"], w=[dkey])

        def mem_attn(mkt, mkkey, mv, mvkey, qT, qkey, c0, n, dstB, dkey):
            for m in range(4):
                mem_attn_head(m, mkt, mkkey, mv, mvkey, qT, qkey, c0, n, dstB, dkey)

        def mem_attn_head(m, mkt, mkkey, mv, mvkey, qT, qkey, c0, n, dstB, dkey):
            sb_ = nb()
            for j in range(2):
                for dc in range(2):
                    S.add("pe", lambda e, j=j, dc=dc: e.matmul(ps[sb_][:, j * n:(j + 1) * n], lhsT=mkt[:, 2 * m + dc, j * 128:(j + 1) * 128], rhs=qT[:, 2 * m + dc, c0:c0 + n], start=(dc == 0), stop=(dc == 1)),
                          r=[mkkey, qkey], w=[f"ps{sb_}"])
            i = eslot[0] % 2
            eslot[0] += 1
            E = Eb[i][:].rearrange("p a b -> p (a b)")[:, 0:2 * n]
            S.add("act", lambda e: e.activation(out=E, in_=ps[sb_][:, 0:2 * n], func=AF.Exp, scale=1.0 / 16), r=[f"ps{sb_}"], w=[f"Eb{i}"])
            zb = nb()
            for j in range(2):
                S.add("pe", lambda e, j=j: e.matmul(ps[zb][:, 0:n], lhsT=onesb[:], rhs=E[:, j * n:(j + 1) * n], start=(j == 0), stop=(j == 1)), r=["onesb", f"Eb{i}"], w=[f"ps{zb}"])
            ob = nb()
            for ec in range(2):
                for j in range(2):
                    S.add("pe", lambda e, j=j, ec=ec: e.matmul(ps[ob][:, ec * n:(ec + 1) * n], lhsT=mv[:, j, (2 * m + ec) * 128:(2 * m + ec + 1) * 128], rhs=E[:, j * n:(j + 1) * n], start=(j == 0), stop=(j == 1)),
                          r=[mvkey, f"Eb{i}"], w=[f"ps{ob}"])
            S.add("dve", lambda e: e.reciprocal(out=T[3][:, 0:n], in_=ps[zb][:, 0:n]), r=[f"ps{zb}"], w=["T3"])
            for ec in range(2):
                S.add("dve", lambda e, ec=ec: e.tensor_tensor(out=dstB[:, 2 * m + ec, c0:c0 + n], in0=ps[ob][:, ec * n:(ec + 1) * n], in1=T[3][:, 0:n], op=ALU.mult),
                      r=[f"ps{ob}", "T3"], w=[dkey])

        for t in range(2):
            front(memx[t * 128:(t + 1) * 128, :], QT, "QT", t * 128)
        chk(41)
        for f0 in range(0, 1024, 256):
            wbuf, wkey = wslab(w_mkv, 0, 8, f0, 256, gain=9)
            for fi in range(2):
                b = mm_fm(wbuf, wkey, 8, fi, QT, "QT", 0, 256)
                S.add("act", lambda e, b=b, fc=f0 // 128 + fi: e.activation(out=MKT[:, fc, :], in_=ps[b][:, 0:256], func=AF.Copy), r=[f"ps{b}"], w=["MKT"])
            for t in range(2):
                b = nb()
                mm_tm(QT, "QT", t * 128, wbuf, wkey, 8, 256, b)
                S.add("act", lambda e, b=b: e.activation(out=stg[:], in_=ps[b][:, 0:256], func=AF.Copy), r=[f"ps{b}"], w=["stg"])
                S.dma("sp", mem_k[t * 128:(t + 1) * 128, f0:f0 + 256], stg[:], r=["stg"], w=["D:mem_k"])
        chk(42)
        for f0 in range(0, 1024, 256):
            wbuf, wkey = wslab(w_mkv, 0, 8, 1024 + f0, 256, gain=9)
            for t in range(2):
                b = nb()
                mm_tm(QT, "QT", t * 128, wbuf, wkey, 8, 256, b)
                S.add("act", lambda e, b=b: e.activation(out=stg[:], in_=ps[b][:, 0:256], func=AF.Copy), r=[f"ps{b}"], w=["stg"])
                S.add("dve", lambda e, b=b, t=t, f0=f0: e.tensor_copy(out=MV[:, t, f0:f0 + 256], in_=ps[b][:, 0:256]), r=[f"ps{b}"], w=["MV"])
                S.dma("sp", mem_v[t * 128:(t + 1) * 128, f0:f0 + 256], stg[:], r=["stg"], w=["D:mem_v"])
        OUTKEYS += ["D:mem_k", "D:mem_v"]
        chk(5)

        def lru_chunk(c, zx_bank_list, npc, has_s, t0, full, zy_src):
            ntot = npc + (128 if has_s else 0)
            S.add("dve", lambda e: e.tensor_copy(out=EXTL[:, 0:3], in_=halo[:, c, :]), r=["halo"], w=["EXTL"])
            for (b, c0, n) in zx_bank_list:
                pc = min(n, max(0, npc - c0))
                if pc > 0:
                    S.add("act", lambda e, b=b, c0=c0, pc=pc: e.activation(out=EXTL[:, 3 + c0:3 + c0 + pc], in_=ps[b][:, 0:pc], func=AF.Copy), r=[f"ps{b}"], w=["EXTL"])
                if pc < n:
                    S.add("act", lambda e, b=b, pc=pc: e.activation(out=EXTS[:, :, 3:11], in_=ps[b][:, pc:pc + 128].rearrange("p (s t) -> p s t", t=8), func=AF.Copy), r=[f"ps{b}"], w=["EXTS"])
            if has_s:
                S.add("dve", lambda e: e.tensor_copy(out=EXTS[:, :, 0:3], in_=stT_c[:, c, :].rearrange("p (s j) -> p s j", j=3)), r=["stT_c"], w=["EXTS"])
            S.add("dve", lambda e: e.tensor_copy(out=halo[:, c, :], in_=EXTL[:, npc:npc + 3]), r=["EXTL"], w=["halo"])
            xc = T[0]
            def conv(dst, src_fn):
                S.add("dve", lambda e: e.tensor_scalar(out=dst, in0=src_fn(0), scalar1=vec8[:, 1, c:c + 1], scalar2=vec8[:, 5, c:c + 1], op0=ALU.mult, op1=ALU.add), r=["EXTL", "EXTS", "vec8"], w=["T0"])
                for j in range(1, 4):
                    S.add("dve", lambda e, j=j: e.scalar_tensor_tensor(out=dst, in0=src_fn(j), scalar=vec8[:, 1 + j, c:c + 1], in1=dst, op0=ALU.mult, op1=ALU.add), r=["EXTL", "EXTS", "vec8", "T0"], w=["T0"])
            conv(xc[:, 0:npc], lambda j: EXTL[:, j:j + npc])
            if has_s:
                conv(xc[:, npc:ntot].rearrange("p (s t) -> p s t", t=8), lambda j: EXTS[:, :, j:j + 8])
            S.add("act", lambda e: e.activation(out=ktb[:, 0:ntot], in_=xc[:, 0:ntot], func=AF.Copy), r=["T0"], w=["ktb"])
            for (wg, wgk, bi, tdst) in ((wa_b, "wa_b", 6, 1), (wi_b, "wi_b", 7, 2)):
                b = nb()
                S.add("pe", lambda e, b=b, wg=wg: e.matmul(ps[b][:, 0:ntot], lhsT=wg[:, c, :], rhs=ktb[:, 0:ntot], start=True, stop=True), r=[wgk, "ktb"], w=[f"ps{b}"])
                S.add("act", lambda e, b=b, bi=bi, tdst=tdst: e.activation(out=T[tdst][:, 0:ntot], in_=ps[b][:, 0:ntot], func=AF.Sigmoid, bias=vec8[:, bi, c:c + 1]), r=[f"ps{b}", "vec8"], w=[f"T{tdst}"])
            S.add("act", lambda e: e.activation(out=T[3][:, 0:ntot], in_=T[1][:, 0:ntot], func=AF.Exp, scale=coef[:, 0, c:c + 1]), r=["T1", "coef"], w=["T3"])
            S.add("act", lambda e: e.activation(out=T[4][:, 0:ntot], in_=T[1][:, 0:ntot], func=AF.Exp, scale=coef[:, 1, c:c + 1]), r=["T1", "coef"], w=["T4"])
            S.add("dve", lambda e: e.tensor_scalar(out=T[4][:, 0:ntot], in0=T[4][:, 0:ntot], scalar1=-1.0, scalar2=1.0, op0=ALU.mult, op1=ALU.add), r=["T4"], w=["T4"])
            S.add("act", lambda e: e.activation(out=T[4][:, 0:ntot], in_=T[4][:, 0:ntot], func=AF.Sqrt), r=["T4"], w=["T4"])
            S.add("dve", lambda e: e.tensor_tensor(out=T[4][:, 0:ntot], in0=T[4][:, 0:ntot], in1=T[2][:, 0:ntot], op=ALU.mult), r=["T4", "T2"], w=["T4"])
            S.add("dve", lambda e: e.tensor_tensor(out=T[4][:, 0:ntot], in0=T[4][:, 0:ntot], in1=xc[:, 0:ntot], op=ALU.mult), r=["T4", "T0"], w=["T4"])
            ntile = npc // 128
            S.add("dve", lambda e: e.tensor_tensor(out=T[4][:, 0:npc].rearrange("p (a b) -> p a b", b=128), in0=T[4][:, 0:npc].rearrange("p (a b) -> p a b", b=128),
                                                    in1=masks[:, 1, t0:t0 + ntile].unsqueeze(2).broadcast_to([128, ntile, 128]), op=ALU.mult), r=["T4", "masks"], w=["T4"])
            S.add("dve", lambda e: e.tensor_tensor_scan(out=T[5][:, 0:npc], data0=T[3][:, 0:npc], data1=T[4][:, 0:npc], initial=hstate[:, c:c + 1], op0=ALU.mult, op1=ALU.add), r=["T3", "T4", "hstate"], w=["T5"])
            S.add("dve", lambda e: e.tensor_copy(out=hstate[:, c:c + 1], in_=T[5][:, npc - 1:npc]), r=["T5"], w=["hstate"])
            if has_s:
                for s in range(16):
                    S.add("dve", lambda e, s=s: e.tensor_tensor_scan(out=T[5][:, npc + s * 8:npc + s * 8 + 8], data0=T[3][:, npc + s * 8:npc + s * 8 + 8], data1=T[4][:, npc + s * 8:npc + s * 8 + 8],
                                                                      initial=stT_h[:, c, s:s + 1], op0=ALU.mult, op1=ALU.add), r=["T3", "T4", "stT_h"], w=["T5"])
                S.add("dve", lambda e: e.tensor_copy(out=hsS[:, c, :], in_=T[5][:, npc:ntot].rearrange("p (s t) -> p s t", t=8)[:, :, 7]), r=["T5"], w=["hsS"])
            if full:
                zb, zk = zy_src
                gelu_mul(BR[0][:, c, 0:ntot], "BR0", zb, zk, T[5][:, 0:ntot], "T5", ntot, 6, 7)

        def run_sg(t0, t1, has_s, full):
            ntile = t1 - t0
            npc = ntile * 128
            ntot = npc + (128 if has_s else 0)
            tiles = [(t, t * 128, (t - t0) * 128) for t in range(t0, t1)]
            if has_s:
                tiles.append(("S", 8192, npc))
            for (t, row, col) in tiles:
                front(xs[row:row + 128, :], hT, "hT", col)
            for f0 in range(0, 1024, 256):
                wbuf, wkey = wslab(w_in, 0, 8, 3072 + f0, 256, gain=0)
                for fi in range(2):
                    h = f0 // 128 + fi
                    b = mm_fm(wbuf, wkey, 8, fi, hT, "hT", 0, ntot)
                    S.add("act", lambda e, b=b: e.activation(out=ktb[:, 0:ntot], in_=ps[b][:, 0:ntot], func=AF.Copy), r=[f"ps{b}"], w=["ktb"])
                    S.dma("sp", KTs[h, :, t0 * 128:t0 * 128 + npc], ktb[:, 0:npc], r=["ktb"], w=[f"D:KTs{h}"])
                    if has_s:
                        S.add("dve", lambda e, h=h: e.tensor_copy(out=KTsm[:, h, :], in_=ktb[:, npc:ntot]), r=["ktb"], w=["KTsm"])
                if full:
                    for (t, row, col) in tiles:
                        if t == 47:
                            continue
                        b = nb()
                        mm_tm(hT, "hT", col, wbuf, wkey, 8, 256, b)
                        S.add("act", lambda e, b=b: e.activation(out=stg[:], in_=ps[b][:, 0:256], func=AF.Copy), r=[f"ps{b}"], w=["stg"])
                        dst = k_s[:, f0:f0 + 256] if t == "S" else k_own[(t - 48) * 128:(t - 47) * 128, f0:f0 + 256]
                        S.dma("sp", dst, stg[:], r=["stg"], w=["D:k"])
            for f0 in range(0, 1024, 256):
                wbuf, wkey = wslab(w_in, 0, 8, 4096 + f0, 256, gain=0)
                for (t, row, col) in tiles:
                    b = nb()
                    mm_tm(hT, "hT", col, wbuf, wkey, 8, 256, b)
                    if t == "S":
                        S.add("dve", lambda e, b=b, f0=f0: e.tensor_copy(out=vbs[:, f0:f0 + 256], in_=ps[b][:, 0:256]), r=[f"ps{b}"], w=["vbs"])
                    else:
                        S.add("dve", lambda e, b=b: e.tensor_copy(out=hb[0][:, 0:256], in_=ps[b][:, 0:256]), r=[f"ps{b}"], w=["hb0"])
                        for fi in range(2):
                            S.dma("sp", Vs[f0 // 128 + fi, t * 128:(t + 1) * 128, :], hb[0][:, fi * 128:(fi + 1) * 128], r=["hb0"], w=[f"D:Vs{f0 // 128 + fi}"])
                    if full and t != 47:
                        S.add("act", lambda e, b=b: e.activation(out=stg[:], in_=ps[b][:, 0:256], func=AF.Copy), r=[f"ps{b}"], w=["stg"])
                        dst = v_s[:, f0:f0 + 256] if t == "S" else v_own[(t - 48) * 128:(t - 47) * 128, f0:f0 + 256]
                        S.dma("sp", dst, stg[:], r=["stg"], w=["D:v"])
            for f0 in range(0, 1024, 256):
                wbx, wkx = wslab(w_in, 0, 8, f0, 256, gain=0)
                if full:
                    wby, wky = wslab(w_in, 0, 8, 1024 + f0, 256, gain=0)
                for fi in range(2):
                    c = f0 // 128 + fi
                    b = mm_fm(wbx, wkx, 8, fi, hT, "hT", 0, ntot)
                    zy = None
                    if full:
                        b2 = mm_fm(wby, wky, 8, fi, hT, "hT", 0, ntot)
                        S.add("act", lambda e, b2=b2: e.activation(out=T[7][:, 0:ntot], in_=ps[b2][:, 0:ntot], func=AF.Copy), r=[f"ps{b2}"], w=["T7"])
                        zy = (T[7][:, 0:ntot], "T7")
                    lru_chunk(c, [(b, 0, ntot)], npc, has_s, t0, full, zy)
                if full and t1 == 64:
                    for (t, row, col) in tiles[-2:]:
                        b = nb()
                        mm_tm(hT, "hT", col, wbx, wkx, 8, 256, b)
                        S.add("act", lambda e, b=b: e.activation(out=stg[:], in_=ps[b][:, 0:256], func=AF.Copy), r=[f"ps{b}"], w=["stg"])
                        if t == "S":
                            for j in range(3):
                                S.dma("sp", lru_conv_s[:, j, f0:f0 + 256], stg[5 + j::8, :], r=["stg"], w=["D:lcs"])
                        else:
                            S.dma("sp", lru_conv_p[:, f0:f0 + 256], stg[125:128, :], r=["stg"], w=["D:lcp"])
            if not full:
                return
            chk(10)
            for f0 in range(0, 1024, 256):
                wbuf, wkey = wslab(w_in, 0, 8, 2048 + f0, 256, gain=0)
                for fi in range(2):
                    b = mm_fm(wbuf, wkey, 8, fi, hT, "hT", 0, ntot)
                    S.add("act", lambda e, b=b, h=f0 // 128 + fi: e.activation(out=QT[:, h, 0:ntot], in_=ps[b][:, 0:ntot], func=AF.Copy), r=[f"ps{b}"], w=["QT"])
            chk(11)
            for h in range(8):
                nk = t1 * 128
                S.dma("sp", KTh[:, 0:nk], KTs[h, :, 0:nk], r=[f"D:KTs{h}"], w=["KTh"])
                S.dma("sp", Vh[:, 0:t1, :], Vs[h, 0:nk, :].rearrange("(t p) e -> p t e", p=128), r=[f"D:Vs{h}"], w=["Vh"])
                for qi, (t, row, col) in enumerate(tiles):
                    if t == "S":
                        continue
                    Ob, Zb = 4, 5
                    for kt in range(t + 1):
                        attn_block(QT[0:64, h, col:col + 128], QT[64:128, h, col:col + 128], "QT",
                                   KTh[0:64, kt * 128:(kt + 1) * 128], KTh[64:128, kt * 128:(kt + 1) * 128], "KTh",
                                   Vh[:, kt, :], "Vh", 128, masks[:, 0, kt:kt + 1], trib[:] if kt == t else None, "trib",
                                   Ob, Zb, 0, kt == 0, kt == t, kt % 2)
                    attn_epilogue(Ob, Zb, 0, 128, BR[1][:, h, col:col + 128], "BR1")
            chk(12)
            if has_s:
                for s in range(16):
                    for zbk in (4, 5):
                        S.add("pe", lambda e, zbk=zbk: e.matmul(ps[zbk][:, 0:128], lhsT=zerob[:], rhs=onesb[:], start=True, stop=False), r=["zerob", "onesb"], w=[f"ps{zbk}"])
                    for pg in range(17):
                        if pg < 16:
                            S.idma(xt[0][:], ck, pidx[:, s * 16 + pg:s * 16 + pg + 1], r=["pidx"], w=["xt0"])
                            S.idma(xt[1][:], cv, pidx[:, s * 16 + pg:s * 16 + pg + 1], r=["pidx"], w=["xt1"])
                            S.add("act", lambda e: e.activation(out=hb[0][:], in_=xt[0][:], func=AF.Copy), r=["xt0"], w=["hb0"])
                            S.add("dve", lambda e: e.tensor_copy(out=hb[1][:], in_=xt[1][:]), r=["xt1"], w=["hb1"])
                            for k in range(8):
                                S.add("pe", lambda e, k=k: e.transpose(out=pstb[:, k * 128:(k + 1) * 128], in_=hb[0][:, k * 128:(k + 1) * 128], identity=identb[:]), r=["hb0", "identb"], w=["ps7"])
                            S.add("dve", lambda e: e.tensor_copy(out=KTp[:].rearrange("p a b -> p (a b)"), in_=pstb), r=["ps7"], w=["KTp"])
                            ktsrc, ktkey, vsrc, vkey, msk = KTp, "KTp", hb[1], "hb1", None
                        else:
                            ktsrc, ktkey, vsrc, vkey, msk = KTsm, "KTsm", vbs, "vbs", smaskb[:, s * 8:s * 8 + 8]
                        for h in range(8):
                            attn_block(QT[0:64, h, npc + s * 8:npc + s * 8 + 8], QT[64:128, h, npc + s * 8:npc + s * 8 + 8], "QT",
                                       ktsrc[0:64, h, :], ktsrc[64:128, h, :], ktkey, vsrc[:, h * 128:(h + 1) * 128], vkey, 8,
                                       0.0, msk, "smaskb", 4, 5, h * 16, False, pg == 16, h % 2)
                    for h in range(8):
                        attn_epilogue(4, 5, h * 16, 8, BR[1][:, h, npc + s * 8:npc + s * 8 + 8], "BR1")
            chk(13)
            for f0 in range(0, 1024, 256):
                wbuf, wkey = wslab(w_in, 0, 8, 5120 + f0, 256, gain=0)
                for fi in range(2):
                    b = mm_fm(wbuf, wkey, 8, fi, hT, "hT", 0, ntot)
                    S.add("act", lambda e, b=b, fc=f0 // 128 + fi: e.activation(out=QT[:, fc, 0:ntot], in_=ps[b][:, 0:ntot], func=AF.Copy), r=[f"ps{b}"], w=["QT"])
            for (t, row, col) in tiles:
                if t == "S":
                    for s in range(16):
                        for (src, dst3, key, tr) in ((cmk, MKTs, "MKTs", True), (cmv, MVs, "MVs", False)):
                            for j in range(2):
                                S.dma("sp", xt[j][:], src[s * 256 + j * 128:s * 256 + (j + 1) * 128, :], w=[f"xt{j}"])
                                if tr:
                                    S.add("act", lambda e, j=j: e.activation(out=hb[j][:], in_=xt[j][:], func=AF.Copy), r=[f"xt{j}"], w=[f"hb{j}"])
                                    for k in range(8):
                                        S.add("pe", lambda e, k=k, j=j: e.transpose(out=pstb[:, k * 128:(k + 1) * 128], in_=hb[j][:, k * 128:(k + 1) * 128], identity=identb[:]), r=[f"hb{j}", "identb"], w=["ps7"])
                                    S.add("dve", lambda e, j=j: e.tensor_copy(out=MKTs[:, :, j * 128:(j + 1) * 128], in_=pstb.rearrange("p (a b) -> p a b", a=8)), r=["ps7"], w=["MKTs"])
                                else:
                                    S.add("dve", lambda e, j=j: e.tensor_copy(out=MVs[:, j, :], in_=xt[j][:]), r=[f"xt{j}"], w=["MVs"])
                        mem_attn(MKTs, "MKTs", MVs, "MVs", QT, "QT", col + s * 8, 8, BR[2], "BR2")
                else:
                    mem_attn(MKT, "MKT", MV, "MV", QT, "QT", col, 128, BR[2], "BR2")
            chk(14)
            for fc in range(8):
                for bidx in range(3):
                    wg, wgk = wslab(w_in, 0, 8, 6144 + bidx * 1024 + fc * 128, 128, gain=0)
                    b = mm_fm(wg, wgk, 8, 0, hT, "hT", 0, ntot)
                    S.add("act", lambda e, b=b: e.activation(out=T[1][:, 0:ntot], in_=ps[b][:, 0:ntot], func=AF.Sigmoid), r=[f"ps{b}"], w=["T1"])
                    wp, wpk = wslab(w_br[bidx], 0, 8, fc * 128, 128)
                    b2 = mm_fm(wp, wpk, 8, 0, BR[bidx], f"BR{bidx}", 0, ntot)
                    if bidx == 0:
                        S.add("dve", lambda e, b2=b2: e.tensor_tensor(out=T[2][:, 0:ntot], in0=ps[b2][:, 0:ntot], in1=T[1][:, 0:ntot], op=ALU.mult), r=[f"ps{b2}", "T1"], w=["T2"])
                    else:
                        S.add("dve", lambda e, b2=b2: e.tensor_tensor(out=T[3][:, 0:ntot], in0=ps[b2][:, 0:ntot], in1=T[1][:, 0:ntot], op=ALU.mult), r=[f"ps{b2}", "T1"], w=["T3"])
                        S.add("dve", lambda e: e.tensor_tensor(out=T[2][:, 0:ntot], in0=T[2][:, 0:ntot], in1=T[3][:, 0:ntot], op=ALU.add), r=["T2", "T3"], w=["T2"])
                S.add("act", lambda e, fc=fc: e.activation(out=QT[:, fc, 0:ntot], in_=T[2][:, 0:ntot], func=AF.Copy), r=["T2"], w=["QT"])
            chk(15)
            for f0 in range(0, 1024, 256):
                wbuf, wkey = wslab(w_out, 0, 8, f0, 256)
                for li, (t, row, col) in enumerate(tiles):
                    b = nb()
                    mm_tm(QT, "QT", col, wbuf, wkey, 8, 256, b)
                    S.add("act", lambda e, b=b, li=li, f0=f0: e.activation(out=YB[:, li, f0:f0 + 256], in_=ps[b][:, 0:256], func=AF.Copy), r=[f"ps{b}"], w=[f"YB{li}"])

            def post_norm_res(li, gi, resid_ap, rkey, out_ap, okey):
                S.add("act", lambda e: e.activation(out=junk[:], in_=YB[:, li, :], func=AF.Square, scale=1.0 / 32, accum_out=ssb[:, 2:3]), r=[f"YB{li}"], w=["junk", "ss2"])
                S.add("act", lambda e: e.activation(out=ssb[:, 2:3], in_=ssb[:, 2:3], func=AF.Sqrt, bias=eps[:, 0:1], scale=1.0), r=["ss2", "eps"], w=["ss2"])
                S.add("dve", lambda e: e.reciprocal(out=ssb[:, 2:3], in_=ssb[:, 2:3]), r=["ss2"], w=["ss2"])
                S.add("dve", lambda e: e.scalar_tensor_tensor(out=YB[:, li, :], in0=YB[:, li, :], scalar=ssb[:, 2:3], in1=bcg[:, gi, :], op0=ALU.mult, op1=ALU.mult), r=[f"YB{li}", "ss2", "bcg"], w=[f"YB{li}"])
                S.add("dve", lambda e: e.tensor_tensor(out=out_ap, in0=YB[:, li, :], in1=resid_ap, op=ALU.add), r=[f"YB{li}", rkey], w=[okey])

            for li, (t, row, col) in enumerate(tiles):
                i = xslot[0] % 2
                xslot[0] += 1
                S.dma("sp", xt[i][:], xs[row:row + 128, :], w=[f"xt{i}"])
                post_norm_res(li, 0, xt[i][:], f"xt{i}", X1[:, li, :], f"X1{li}")
                front(None, hT, "hT", col, xkeep=(X1[:, li, :], f"X1{li}"))
            chk(16)
            for c in range(24):
                wgb, wgk = wslab(w_up, 0, 8, c * 128, 128, gain=10)
                wvb, wvk = wslab(w_up, 0, 8, 3072 + c * 128, 128, gain=10)
                for (wbuf_, wkey_, EX, EXS, exk, cc) in ((wgb, wgk, EXTG, EXSG, "EXTG", c), (wvb, wvk, EXTV, EXSV, "EXTV", 24 + c)):
                    b = mm_fm(wbuf_, wkey_, 8, 0, hT, "hT", 0, ntot)
                    S.add("dve", lambda e, EX=EX, cc=cc: e.tensor_copy(out=EX[:, 0:2], in_=FH[:, cc, :]), r=["FH"], w=[exk])
                    S.add("act", lambda e, b=b, EX=EX: e.activation(out=EX[:, 2:2 + npc], in_=ps[b][:, 0:npc], func=AF.Copy), r=[f"ps{b}"], w=[exk])
                    if t0 == 47:
                        S.add("dve", lambda e, EX=EX: e.tensor_scalar_mul(out=EX[:, 0:130], in0=EX[:, 0:130], scalar1=masks[:, 1, 47:48]), r=[exk, "masks"], w=[exk])
                    S.add("dve", lambda e, EX=EX, cc=cc: e.tensor_copy(out=FH[:, cc, :], in_=EX[:, npc:npc + 2]), r=[exk], w=["FH"])
                    if has_s:
                        S.add("act", lambda e, b=b, EXS=EXS: e.activation(out=EXS[:, :, 2:10], in_=ps[b][:, npc:ntot].rearrange("p (s t) -> p s t", t=8), func=AF.Copy), r=[f"ps{b}"], w=[exk])
                        S.add("dve", lambda e, EXS=EXS, cc=cc: e.tensor_copy(out=EXS[:, :, 0:2], in_=stT_f[:, cc, :].rearrange("p (s j) -> p s j", j=2)), r=["stT_f"], w=[exk])
                    tdst = 5 if cc < 24 else 6

                    def conv3(dst, src_fn, cc=cc, exk=exk, tdst=tdst):
                        S.add("dve", lambda e: e.tensor_scalar(out=dst, in0=src_fn(0), scalar1=ffnv[:, 0, cc:cc + 1], scalar2=ffnv[:, 3, cc:cc + 1], op0=ALU.mult, op1=ALU.add), r=[exk, "ffnv"], w=[f"T{tdst}"])
                        for j in range(1, 3):
                            S.add("dve", lambda e, j=j: e.scalar_tensor_tensor(out=dst, in0=src_fn(j), scalar=ffnv[:, j, cc:cc + 1], in1=dst, op0=ALU.mult, op1=ALU.add), r=[exk, "ffnv", f"T{tdst}"], w=[f"T{tdst}"])
                    conv3(T[tdst][:, 0:npc], lambda j, EX=EX: EX[:, j:j + npc])
                    if has_s:
                        conv3(T[tdst][:, npc:ntot].rearrange("p (s t) -> p s t", t=8), lambda j, EXS=EXS: EXS[:, :, j:j + 8])
                    if t1 == 64:
                        for (t, row, col) in tiles[-2:]:
                            b3 = nb()
                            mm_tm(hT, "hT", col, wbuf_, wkey_, 8, 128, b3)
                            S.add("act", lambda e, b3=b3: e.activation(out=stg[:, 0:128], in_=ps[b3][:, 0:128], func=AF.Copy), r=[f"ps{b3}"], w=["stg"])
                            if t == "S":
                                for j in range(2):
                                    S.dma("sp", ffn_conv_s[:, j, cc * 128:(cc + 1) * 128], stg[6 + j::8, 0:128], r=["stg"], w=["D:fcs"])
                            else:
                                S.dma("sp", ffn_conv_p[:, cc * 128:(cc + 1) * 128], stg[126:128, 0:128], r=["stg"], w=["D:fcp"])
                gelu_mul(BR[c // 8][:, c % 8, 0:ntot], f"BR{c // 8}", T[5][:, 0:ntot], "T5", T[6][:, 0:ntot], "T6", ntot, 4, 7)
            chk(17)
            for f0 in range(0, 1024, 256):
                banks = [nb() for _ in tiles]
                for g in range(3):
                    wbuf, wkey = wslab(w_down, g * 8, 8, f0, 256)
                    for li, (t, row, col) in enumerate(tiles):
                        mm_tm(BR[g], f"BR{g}", col, wbuf, wkey, 8, 256, banks[li], start=(g == 0), stop=(g == 2))
                for li, (t, row, col) in enumerate(tiles):
                    b = banks[li]
                    S.add("act", lambda e, b=b, li=li, f0=f0: e.activation(out=YB[:, li, f0:f0 + 256], in_=ps[b][:, 0:256], func=AF.Copy), r=[f"ps{b}"], w=[f"YB{li}"])
            for li, (t, row, col) in enumerate(tiles):
                post_norm_res(li, 1, X1[:, li, :], f"X1{li}", YB[:, li, :], f"YB{li}")
                if t == 47:
                    continue
                dst = y_s[:, :] if t == "S" else y_own[(t - 48) * 128:(t - 47) * 128, :]
                S.dma("sp", dst, YB[:, li, :], r=[f"YB{li}"], w=["D:y"])

        for (a, b_) in light_sgs:
            run_sg(a, b_, False, False)
        for (a, b_) in full_sgs:
            run_sg(a, b_, b_ == 64, True)
        chk(6)
        S.muted = (dbg == 7)
        S.dma("sp", lru_h_p.rearrange("o (c p) -> p (o c)", p=128), hstate[:], r=["hstate"], w=["D:lhp"])
        for c in range(8):
            S.dma("sp", lru_h_s[:, c * 128:(c + 1) * 128].rearrange("s p -> p s"), hsS[:, c, :], r=["hsS"], w=["D:lhs"])
        OUTKEYS += ["D:k", "D:v", "D:lcs", "D:lcp", "D:fcs", "D:fcp", "D:y", "D:lhp", "D:lhs"]
        S.muted = False
        S.add("sp", None, r=OUTKEYS)
        S.emit()
    return nc


def _lay8(v):
    return np.ascontiguousarray(np.asarray(v, np.float32).reshape(-1, 128).T)


def make_in_maps(inp, with_cache=True):
    f = lambda k: np.asarray(inp[k])
    n = 8
    vec8 = np.stack([_lay8(f("g_pre_mix")[0])] + [_lay8(f("w_lru_conv")[0, j]) for j in range(4)] +
                    [_lay8(f(k)[0]) for k in ("b_lru_conv", "b_rg_a", "b_rg_i", "lru_lambda", "g_mem", "g_pre_ffn")], axis=1)
    ffnv = np.stack([_lay8(f("w_ffn_conv")[0, j]) for j in range(3)] + [_lay8(f("b_ffn_conv")[0])], axis=1)
    bc = np.ascontiguousarray(np.broadcast_to(np.stack([f("g_post_mix")[0], f("g_post_ffn")[0]])[None], (128, 2, 1024)))
    lamv = np.ascontiguousarray(np.broadcast_to(np.stack([f("lambda_q1")[0], f("lambda_k1")[0], f("lambda_q2")[0], f("lambda_k2")[0]])[None], (128, 4, 64)))
    gsub = np.ascontiguousarray(f("g_subln")[0].reshape(128, 1))
    ident = np.eye(128, dtype=np.float32)
    tri = np.triu(np.ones((128, 128), np.float32))
    pp = np.arange(128)
    smask = ((pp[:, None] // 8 == pp[None, :] // 8) & (pp[:, None] % 8 <= pp[None, :] % 8)).astype(np.float32)
    consts = np.ascontiguousarray(np.stack([ident, tri, smask], axis=1))
    iota = np.arange(128, dtype=np.float32).reshape(128, 1)
    shared = {
        "w_in": f("w_in")[0], "w_rga": f("w_rg_a")[0].reshape(1024, 128), "w_rgi": f("w_rg_i")[0].reshape(1024, 128),
        "w_mkv": f("w_mem_kv")[0], "w_br0": f("w_br_lru")[0], "w_br1": f("w_br_attn")[0], "w_br2": f("w_br_mem")[0],
        "w_out": f("w_out")[0], "w_up": f("w_up")[0], "w_down": f("w_down")[0],
        "vec8": vec8, "ffnv": ffnv, "bc": bc, "lamv": lamv, "gsub": gsub, "consts": consts, "iota": iota,
    }
    if with_cache:
        shared["ck"] = f("cache_k")[0].reshape(-1, 1024)
        shared["cv"] = f("cache_v")[0].reshape(-1, 1024)
    xp = f("x_prompt"); xsmp = f("x_sample").reshape(1024, 1024)
    in_maps = []
    for c in range(n):
        b, j = divmod(c, 4)
        xs = np.zeros((8320, 1024), np.float32)
        nreal = (j + 1) * 2048
        xs[8192 - nreal:8192] = xp[b, :nreal]
        xs[8192:] = xsmp[c * 128:(c + 1) * 128]
        npad_t = (8192 - nreal) // 128
        masks = np.zeros((128, 2, 64), np.float32)
        masks[:, 0, :npad_t] = -30000.0
        masks[:, 1, npad_t:] = 1.0
        sl = slice(c * 16, (c + 1) * 16)
        m = dict(shared)
        m.update({
            "xs": xs, "memx": f("mem_prompt")[b], "pt": f("page_table")[sl].reshape(1, 256).astype(np.int32),
            "cmk": f("cache_mem_k")[0, sl].reshape(4096, 1024), "cmv": f("cache_mem_v")[0, sl].reshape(4096, 1024),
            "st_h": f("state_lru_h")[0, sl], "st_c": f("state_lru_conv")[0, sl].reshape(48, 1024),
            "st_f": f("state_ffn_conv")[0, sl].reshape(32, 6144), "masks": masks,
        })
        in_maps.append(m)
    return in_maps


def launch(in_maps, **bkw):
    nc = build(**bkw)
    return run_bass_kernel_spmd(nc, in_maps, core_ids=list(range(len(in_maps)))).results


def assemble(res):
    cat = lambda k, cs: np.concatenate([res[c][k] for c in cs], axis=0)
    y_p = np.stack([cat("y_own", range(4 * b, 4 * b + 4)) for b in range(2)])
    k_p = np.stack([cat("k_own", range(4 * b, 4 * b + 4)) for b in range(2)]).reshape(1, 2, 8192, 8, 2, 64)
    v_p = np.stack([cat("v_own", range(4 * b, 4 * b + 4)) for b in range(2)]).reshape(1, 2, 8192, 8, 128)
    outs = (
        y_p, cat("y_s", range(8)).reshape(128, 8, 1024), k_p, v_p,
        np.stack([res[3 + 4 * b]["mem_k"] for b in range(2)]).reshape(1, 2, 256, 4, 256),
        np.stack([res[3 + 4 * b]["mem_v"] for b in range(2)]).reshape(1, 2, 256, 4, 256),
        np.stack([res[3 + 4 * b]["lru_h_p"][0] for b in range(2)]).reshape(1, 2, 1024),
        np.stack([res[3 + 4 * b]["lru_conv_p"] for b in range(2)]).reshape(1, 2, 3, 1024),
        np.stack([res[3 + 4 * b]["ffn_conv_p"] for b in range(2)]).reshape(1, 2, 2, 6144),
        cat("k_s", range(8)).reshape(1, 128, 8, 8, 2, 64), cat("v_s", range(8)).reshape(1, 128, 8, 8, 128),
        cat("lru_h_s", range(8)).reshape(1, 128, 1024), cat("lru_conv_s", range(8)).reshape(1, 128, 3, 1024),
        cat("ffn_conv_s", range(8)).reshape(1, 128, 2, 6144),
    )
    return tuple(np.ascontiguousarray(o, dtype=np.float32) for o in outs)


def kernel(**inp):
    return assemble(launch(make_in_maps(inp)))
```

```python
import contextlib
import numpy as np
import concourse.bass as bass
import concourse.mybir as mybir
from concourse.bass_utils import run_bass_kernel_spmd

F32 = mybir.dt.float32
BF16 = mybir.dt.bfloat16
I32 = mybir.dt.int32
AF = mybir.ActivationFunctionType
ALU = mybir.AluOpType
AX = mybir.AxisListType


class _Op:
    __slots__ = ("eng", "fn", "deps", "is_dma", "signal", "semkey", "sem", "val", "idx")


class Sched:
    INORDER = {"pe"}

    def __init__(self, nc, stack, n_dma_sems=80):
        self.nc = nc
        self.ops = []
        self.last_w = {}
        self.readers = {}
        self.engs = {"pe": nc.tensor, "act": nc.scalar, "dve": nc.vector, "pool": nc.gpsimd, "sp": nc.sync}
        self.eng_sems = {e: [stack.enter_context(nc.semaphore(f"c_{e}"))] for e in self.engs}
        self.free_dma_sems = [stack.enter_context(nc.semaphore(f"d{i}")) for i in range(n_dma_sems)]
        self.dma_sem = {}
        self.muted = False

    def _mk(self, eng, fn, r, w, is_dma, semkey=None):
        if self.muted:
            return None
        w = tuple(w) + tuple(k for k in r if str(k).startswith("ps"))
        r = tuple(k for k in r if not str(k).startswith("ps"))
        op = _Op()
        op.eng, op.fn, op.is_dma, op.signal, op.semkey = eng, fn, is_dma, bool(is_dma), semkey
        op.sem = None
        op.val = 0
        op.idx = len(self.ops)
        deps = {}
        for k in r:
            d = self.last_w.get(k)
            if d is not None:
                deps[d.idx] = d
        for k in w:
            d = self.last_w.get(k)
            if d is not None:
                deps[d.idx] = d
            for d in self.readers.get(k, ()):
                deps[d.idx] = d
        need = []
        for d in deps.values():
            if d is op:
                continue
            if d.is_dma or is_dma or d.eng != eng or eng not in self.INORDER:
                d.signal = True
                need.append(d)
        op.deps = need
        for k in w:
            self.last_w[k] = op
            self.readers[k] = []
        for k in r:
            if k in w:
                continue
            self.readers.setdefault(k, []).append(op)
        self.ops.append(op)
        return op

    def add(self, eng, fn, r=(), w=()):
        return self._mk(eng, fn, tuple(r), tuple(w), False)

    def dma(self, q, out, in_, r=(), w=(), semkey=None, **kw):
        if semkey is None:
            semkey = w[0] if (w and not str(w[0]).startswith("D:")) else r[0]

        def fn(e, out=out, in_=in_, kw=kw):
            return e.dma_start(out=out, in_=in_, **kw)
        return self._mk(q, fn, tuple(r), tuple(w), True, semkey)

    def idma(self, out, in_, idx_ap, r=(), w=()):
        def fn(e):
            return e.indirect_dma_start(out=out, out_offset=None, in_=in_,
                                        in_offset=bass.IndirectOffsetOnAxis(ap=idx_ap, axis=0))
        return self._mk("pool", fn, tuple(r), tuple(w), True, w[0])

    def emit(self):
        nc = self.nc
        cnt = {e: 0 for e in self.engs}
        dcnt = {}
        for op in self.ops:
            if not op.signal:
                continue
            if op.is_dma:
                if op.semkey not in self.dma_sem:
                    self.dma_sem[op.semkey] = self.free_dma_sems.pop()
                op.sem = self.dma_sem[op.semkey]
                dcnt[op.semkey] = dcnt.get(op.semkey, 0) + 16
                op.val = dcnt[op.semkey]
            else:
                cnt[op.eng] += 1
                if cnt[op.eng] > 30000:
                    self.eng_sems[op.eng].append(self.free_dma_sems.pop())
                    cnt[op.eng] = 1
                op.sem = self.eng_sems[op.eng][-1]
                op.val = cnt[op.eng]
        per_eng = {e: [op for op in self.ops if op.eng == e] for e in self.engs}

        def run(ename, engine):
            waited = {}
            for op in per_eng[ename]:
                for d in op.deps:
                    key = id(d.sem)
                    if waited.get(key, 0) < d.val:
                        engine.wait_ge(d.sem, d.val)
                        waited[key] = d.val
                if op.fn is None:
                    continue
                ins = op.fn(engine)
                if op.signal:
                    ins.then_inc(op.sem, 16 if op.is_dma else 1)

        with nc.Block() as block:
            @block.tensor
            def _(e):
                run("pe", e)

            @block.scalar
            def _(e):
                run("act", e)

            @block.vector
            def _(e):
                run("dve", e)

            @block.gpsimd
            def _(e):
                run("pool", e)

            @block.sync
            def _(e):
                run("sp", e)


NPRE = 47
TS = 3
NCOL = TS * 128
LIGHT_SGS = [(t, min(t + TS, NPRE)) for t in range(0, NPRE, TS)]
FULL_SGS = [(47, 50), (50, 53), (53, 56), (56, 59), (59, 62), (62, 64)]
NPOOL = 2560
GK = 1.5957691216057308


class _Stop(Exception):
    pass


def build(light_sgs=LIGHT_SGS, full_sgs=FULL_SGS, dbg=99):
    nc = bass.Bass("TRN2", target_bir_lowering=False)
    din = lambda n, s, d=F32: nc.dram_tensor(n, list(s), d, kind="ExternalInput").ap()
    dout = lambda n, s, d=F32: nc.dram_tensor(n, list(s), d, kind="ExternalOutput").ap()
    xs = din("xs", [8320, 1024]); memx = din("memx", [256, 1024])
    ck = din("ck", [NPOOL * 128, 1024]); cv = din("cv", [NPOOL * 128, 1024])
    pt = din("pt", [1, 256], I32)
    cmk = din("cmk", [4096, 1024]); cmv = din("cmv", [4096, 1024])
    st_h = din("st_h", [16, 1024]); st_c = din("st_c", [48, 1024]); st_f = din("st_f", [32, 6144])
    w_in = din("w_in", [1024, 9216]); w_rga = din("w_rga", [1024, 128]); w_rgi = din("w_rgi", [1024, 128])
    w_mkv = din("w_mkv", [1024, 2048])
    w_br = [din(f"w_br{i}", [1024, 1024]) for i in range(3)]
    w_out = din("w_out", [1024, 1024]); w_up = din("w_up", [1024, 6144]); w_down = din("w_down", [3072, 1024])
    vec8_d = din("vec8", [128, 11, 8]); ffnv_d = din("ffnv", [128, 4, 48]); bc_d = din("bc", [128, 2, 1024])
    lamv_d = din("lamv", [128, 4, 64]); gsub_d = din("gsub", [128, 1]); consts_d = din("consts", [128, 3, 128])
    iota_d = din("iota", [128, 1]); masks_d = din("masks", [128, 2, 64])

    y_own = dout("y_own", [2048, 1024]); y_s = dout("y_s", [128, 1024])
    k_own = dout("k_own", [2048, 1024]); v_own = dout("v_own", [2048, 1024])
    k_s = dout("k_s", [128, 1024]); v_s = dout("v_s", [128, 1024])
    mem_k = dout("mem_k", [256, 1024]); mem_v = dout("mem_v", [256, 1024])
    lru_h_p = dout("lru_h_p", [1, 1024]); lru_conv_p = dout("lru_conv_p", [3, 1024]); ffn_conv_p = dout("ffn_conv_p", [2, 6144])
    lru_h_s = dout("lru_h_s", [16, 1024]); lru_conv_s = dout("lru_conv_s", [16, 3, 1024]); ffn_conv_s = dout("ffn_conv_s", [16, 2, 6144])
    KTs = nc.dram_tensor("KTs", [8, 128, 8192], BF16, kind="Internal").ap()
    Vs = nc.dram_tensor("Vs", [8, 8192, 128], BF16, kind="Internal").ap()
    OUTKEYS = []

    with contextlib.ExitStack() as st:
        S = Sched(nc, st)
        st.enter_context(nc.allow_non_contiguous_dma("small strided state/layout DMAs"))
        sb = lambda n, s, d=F32: st.enter_context(nc.sbuf_tensor("s_" + n, list(s), d))
        ps = [st.enter_context(nc.psum_tensor(f"ps{i}", [128, 512], F32)) for i in range(7)]
        pstb_t = st.enter_context(nc.psum_tensor("ps7", [128, 1024], BF16))
        pstb = pstb_t[:]

        consts = sb("consts", [128, 3, 128]); identb = sb("identb", [128, 128], BF16)
        trib = sb("trib", [128, 128], BF16); smaskb = sb("smaskb", [128, 128], BF16)
        onesb = sb("onesb", [128, 128], BF16); onesf = sb("onesf", [128, 128]); zerob = sb("zerob", [128, 128], BF16)
        vec8 = sb("vec8", [128, 11, 8]); ffnv = sb("ffnv", [128, 4, 48]); bcg = sb("bcg", [128, 2, 1024])
        lamv = sb("lamv", [128, 4, 64]); gsub = sb("gsub", [128, 1]); iota = sb("iota", [128, 1])
        masks = sb("masks", [128, 2, 64]); eps = sb("eps", [128, 1]); small = sb("small", [128, 16])
        coef = sb("coef", [128, 2, 8]); lamt = sb("lamt", [128, 64])
        pti = sb("pti", [128, 256], I32); ptf = sb("ptf", [128, 256]); pidx = sb("pidx", [128, 256], I32)
        wa_b = sb("wa_b", [128, 8, 128], BF16); wi_b = sb("wi_b", [128, 8, 128], BF16)
        xt = [sb(f"xt{i}", [128, 1024]) for i in range(2)]
        junk = sb("junk", [128, 1024], BF16)
        hb = [sb(f"hb{i}", [128, 1024], BF16) for i in range(2)]
        ssb = sb("ssb", [128, 4]);
        hT = sb("hT", [128, 8, NCOL], BF16); QT = sb("QT", [128, 8, NCOL], BF16)
        BR = [sb(f"BR{i}", [128, 8, NCOL], BF16) for i in range(3)]
        wf = [sb(f"wf{i}", [128, 8, 256]) for i in range(2)]
        wb = [sb(f"wb{i}", [128, 8, 256], BF16) for i in range(3)]
        T = [sb(f"T{i}", [128, NCOL]) for i in range(8)]
        EXTL = sb("EXTL", [128, 3 + NCOL]); EXTS = sb("EXTS", [128, 16, 11])
        EXTG = sb("EXTG", [128, 2 + NCOL]); EXTV = sb("EXTV", [128, 2 + NCOL])
        EXSG = sb("EXSG", [128, 16, 10]); EXSV = sb("EXSV", [128, 16, 10])
        KTh = sb("KTh", [128, 8192], BF16); Vh = sb("Vh", [128, 64, 128], BF16)
        Eb = [sb(f"Eb{i}", [128, 2, 128], BF16) for i in range(2)]
        KTp = sb("KTp", [128, 8, 128], BF16)
        MKT = sb("MKT", [128, 8, 256], BF16); MV = sb("MV", [128, 2, 1024], BF16)
        MKTs = sb("MKTs", [128, 8, 256], BF16); MVs = sb("MVs", [128, 2, 1024], BF16)
        ktb = sb("ktb", [128, NCOL], BF16); vbs = sb("vbs", [128, 1024], BF16)
        KTsm = sb("KTsm", [128, 8, 128], BF16)
        X1 = sb("X1", [128, TS, 1024]); YB = sb("YB", [128, TS, 1024])
        stT_c = sb("stT_c", [128, 8, 48]); stT_h = sb("stT_h", [128, 8, 16]); stT_f = sb("stT_f", [128, 48, 32])
        halo = sb("halo", [128, 8, 3]); hstate = sb("hstate", [128, 8]); FH = sb("FH", [128, 48, 2])
        hsS = sb("hsS", [128, 8, 16]); stg = sb("stg", [128, 256])

        bank = [0]

        def nb():
            b = bank[0] % 7
            bank[0] += 1
            return b

        def chk(n):
            if dbg == n:
                S.muted = True
        S.dma("sp", consts[:], consts_d, w=["consts"])
        S.dma("sp", vec8[:], vec8_d, w=["vec8"]); S.dma("sp", ffnv[:], ffnv_d, w=["ffnv"])
        S.dma("sp", bcg[:], bc_d, w=["bcg"]); S.dma("sp", lamv[:], lamv_d, w=["lamv"])
        S.dma("sp", gsub[:], gsub_d, w=["gsub"]); S.dma("sp", iota[:], iota_d, w=["iota"])
        S.dma("sp", masks[:], masks_d, w=["masks"])
        S.dma("sp", pti[:], pt.partition_broadcast(128), w=["pti"])
        chk(1)
        S.add("dve", lambda e: e.tensor_copy(out=identb[:], in_=consts[:, 0, :]), r=["consts"], w=["identb"])
        S.add("dve", lambda e: e.tensor_copy(out=trib[:], in_=consts[:, 1, :]), r=["consts"], w=["trib"])
        S.add("dve", lambda e: e.tensor_copy(out=smaskb[:], in_=consts[:, 2, :]), r=["consts"], w=["smaskb"])
        S.add("dve", lambda e: e.memset(onesb[:], 1.0), w=["onesb"])
        S.add("dve", lambda e: e.memset(onesf[:], 1.0), w=["onesf"])
        S.add("dve", lambda e: e.memset(zerob[:], 0.0), w=["zerob"])
        S.add("dve", lambda e: e.memset(eps[:], 1e-6), w=["eps"])
        S.add("dve", lambda e: e.memset(halo[:], 0.0), w=["halo"])
        S.add("dve", lambda e: e.memset(hstate[:], 0.0), w=["hstate"])
        S.add("dve", lambda e: e.memset(FH[:], 0.0), w=["FH"])
        S.add("act", lambda e: e.activation(out=coef[:, 0, :], in_=vec8[:, 8, :], func=AF.Exp, scale=-1.0), r=["vec8"], w=["coef"])
        S.add("dve", lambda e: e.tensor_scalar_add(out=coef[:, 0, :], in0=coef[:, 0, :], scalar1=1.0), r=["coef"], w=["coef"])
        S.add("act", lambda e: e.activation(out=coef[:, 0, :], in_=coef[:, 0, :], func=AF.Ln), r=["coef"], w=["coef"])
        S.add("dve", lambda e: e.tensor_scalar_mul(out=coef[:, 1, :], in0=coef[:, 0, :], scalar1=-16.0), r=["coef"], w=["coef"])
        S.add("dve", lambda e: e.tensor_scalar_mul(out=coef[:, 0, :], in0=coef[:, 0, :], scalar1=-8.0), r=["coef"], w=["coef"])
        for i in range(2):
            S.add("dve", lambda e, i=i: e.tensor_tensor(out=lamt[:], in0=lamv[:, 2 * i, :], in1=lamv[:, 2 * i + 1, :], op=ALU.mult), r=["lamv"], w=["lamt"])
            S.add("dve", lambda e, i=i: e.reduce_sum(out=small[:, 2 + i:3 + i], in_=lamt[:], axis=AX.X), r=["lamt"], w=["small"])
        S.add("act", lambda e: e.activation(out=small[:, 2:4], in_=small[:, 2:4], func=AF.Exp), r=["small"], w=["small"])
        S.add("dve", lambda e: e.tensor_tensor(out=small[:, 0:1], in0=small[:, 3:4], in1=small[:, 2:3], op=ALU.subtract), r=["small"], w=["small"])
        S.add("dve", lambda e: e.tensor_scalar_add(out=small[:, 0:1], in0=small[:, 0:1], scalar1=-0.2), r=["small"], w=["small"])
        S.add("dve", lambda e: e.tensor_scalar_mul(out=small[:, 1:2], in0=gsub[:], scalar1=0.8), r=["small", "gsub"], w=["small"])
        neglam = small[:, 0:1]; gsub08 = small[:, 1:2]
        S.add("dve", lambda e: e.tensor_copy(out=ptf[:], in_=pti[:]), r=["pti"], w=["ptf"])
        S.add("dve", lambda e: e.tensor_scalar(out=ptf[:], in0=ptf[:], scalar1=128.0, scalar2=iota[:, 0:1], op0=ALU.mult, op1=ALU.add), r=["ptf", "iota"], w=["ptf"])
        S.add("dve", lambda e: e.tensor_copy(out=pidx[:], in_=ptf[:]), r=["ptf"], w=["pidx"])
        chk(2)
        for (wd, wdst, nm) in ((w_rga, wa_b, "wa_b"), (w_rgi, wi_b, "wi_b")):
            S.dma("sp", wf[0][:, :, 0:128], wd.rearrange("(n c) d -> c n d", c=128), w=["wf0"])
            S.add("dve", lambda e, wdst=wdst: e.tensor_copy(out=wdst[:], in_=wf[0][:, :, 0:128]), r=["wf0"], w=[nm])

        chk(3)
        def tr_state(src, nrows, ncols, dst3, key):
            for c0 in range(0, ncols, 1024):
                S.dma("sp", xt[0][:nrows, :], src[:, c0:c0 + 1024], w=["xt0"])
                for k in range(8):
                    S.add("pe", lambda e, k=k: e.transpose(out=ps[6][:, k * nrows:(k + 1) * nrows], in_=xt[0][:nrows, k * 128:(k + 1) * 128], identity=consts[:nrows, 0, :nrows]),
                          r=["xt0", "consts"], w=["ps6"])
                S.add("dve", lambda e, c0=c0: e.tensor_copy(out=dst3[:, c0 // 128:c0 // 128 + 8, :], in_=ps[6][:, 0:8 * nrows].rearrange("p (a b) -> p a b", a=8)),
                      r=["ps6"], w=[key])
        tr_state(st_c, 48, 1024, stT_c, "stT_c")
        tr_state(st_h, 16, 1024, stT_h, "stT_h")
        tr_state(st_f, 32, 6144, stT_f, "stT_f")

        chk(4)
        xslot = [0]

        def front(src_rows, dstT, dkey, col0, xkeep=None):
            i = xslot[0] % 2
            xslot[0] += 1
            if xkeep is None:
                S.dma("sp", xt[i][:], src_rows, w=[f"xt{i}"])
                xin, xkey = xt[i][:], f"xt{i}"
            else:
                xin, xkey = xkeep
            S.add("act", lambda e: e.activation(out=junk[:], in_=xin, func=AF.Square, scale=1.0 / 32, accum_out=ssb[:, i:i + 1]), r=[xkey], w=["junk", f"ss{i}"])
            S.add("act", lambda e: e.activation(out=ssb[:, i:i + 1], in_=ssb[:, i:i + 1], func=AF.Sqrt, bias=eps[:, 0:1], scale=1.0), r=[f"ss{i}", "eps"], w=[f"ss{i}"])
            S.add("dve", lambda e: e.reciprocal(out=ssb[:, i:i + 1], in_=ssb[:, i:i + 1]), r=[f"ss{i}"], w=[f"ss{i}"])
            S.add("act", lambda e: e.activation(out=hb[i][:], in_=xin, func=AF.Copy, scale=ssb[:, i:i + 1]), r=[xkey, f"ss{i}"], w=[f"hb{i}"])
            for k in range(8):
                S.add("pe", lambda e, k=k: e.transpose(out=pstb[:, k * 128:(k + 1) * 128], in_=hb[i][:, k * 128:(k + 1) * 128], identity=identb[:]),
                      r=[f"hb{i}", "identb"], w=["ps7"])
            S.add("dve", lambda e: e.tensor_copy(out=dstT[:, :, col0:col0 + 128], in_=pstb.rearrange("p (a b) -> p a b", a=8)), r=["ps7"], w=[dkey])
            return i

        wslot = [0]

        def wslab(W, kc0, KC, f0, FW, gain=None):
            i = wslot[0] % 3
            fi_ = wslot[0] % 2
            wslot[0] += 1
            src = W[kc0 * 128:(kc0 + KC) * 128, f0:f0 + FW].rearrange("(kc p) f -> p kc f", p=128)
            S.dma("sp", wf[fi_][:, :KC, :FW], src, w=[f"wf{fi_}"])
            if gain is None:
                S.add("pool", lambda e: e.tensor_copy(out=wb[i][:, :KC, :FW], in_=wf[fi_][:, :KC, :FW]), r=[f"wf{fi_}"], w=[f"wb{i}"])
            else:
                S.add("dve", lambda e: e.tensor_tensor(out=wb[i][:, :KC, :FW], in0=wf[fi_][:, :KC, :FW],
                                                        in1=vec8[:, gain, kc0:kc0 + KC].unsqueeze(2).broadcast_to([128, KC, FW]), op=ALU.mult),
                      r=[f"wf{fi_}", "vec8"], w=[f"wb{i}"])
            return wb[i], f"wb{i}"

        def mm_fm(wbuf, wkey, KC, fi, actT, akey, c0, n, bank_=None, extra_r=()):
            b = nb() if bank_ is None else bank_
            for kc in range(KC):
                S.add("pe", lambda e, kc=kc: e.matmul(ps[b][:, :n], lhsT=wbuf[:, kc, fi * 128:(fi + 1) * 128], rhs=actT[:, kc, c0:c0 + n], start=(kc == 0), stop=(kc == KC - 1)),
                      r=[wkey, akey] + list(extra_r), w=[f"ps{b}"])
            return b

        def mm_tm(actT, akey, c0, wbuf, wkey, KC, FW, b, foff=0, start=True, stop=True):
            for kc in range(KC):
                S.add("pe", lambda e, kc=kc: e.matmul(ps[b][:, foff:foff + FW], lhsT=actT[:, kc, c0:c0 + 128], rhs=wbuf[:, kc, :FW], start=(start and kc == 0), stop=(stop and kc == KC - 1)),
                      r=[wkey, akey], w=[f"ps{b}"])

        def gelu_mul(dst, dkey, xin, xkey, other, okey, n, tA, tB):
            S.add("act", lambda e: e.activation(out=T[tA][:, :n], in_=xin, func=AF.Square), r=[xkey], w=[f"T{tA}"])
            S.add("dve", lambda e: e.tensor_scalar(out=T[tA][:, :n], in0=T[tA][:, :n], scalar1=0.044715, scalar2=1.0, op0=ALU.mult, op1=ALU.add), r=[f"T{tA}"], w=[f"T{tA}"])
            S.add("dve", lambda e: e.tensor_tensor(out=T[tA][:, :n], in0=T[tA][:, :n], in1=xin, op=ALU.mult), r=[f"T{tA}", xkey], w=[f"T{tA}"])
            S.add("act", lambda e: e.activation(out=T[tA][:, :n], in_=T[tA][:, :n], func=AF.Sigmoid, scale=GK), r=[f"T{tA}"], w=[f"T{tA}"])
            S.add("dve", lambda e: e.tensor_tensor(out=T[tA][:, :n], in0=T[tA][:, :n], in1=xin, op=ALU.mult), r=[f"T{tA}", xkey], w=[f"T{tA}"])
            S.add("dve", lambda e: e.tensor_tensor(out=dst, in0=T[tA][:, :n], in1=other, op=ALU.mult), r=[f"T{tA}", okey], w=[dkey])

        eslot = [0]

        def attn_block(qap0, qap1, qkey, kt0, kt1, kkey, vap, vkey, n, bias_ap, mask_ap, mkey, Ob, Zb, ocol, first, last, sb_):
            st_ = {}

            def phase_s():
                i = eslot[0] % 2
                eslot[0] += 1
                st_["i"] = i
                sb2_ = sb_ + 2
                S.add("pe", lambda e: e.matmul(ps[sb_][:, 0:n], lhsT=kt0, rhs=qap0, start=True, stop=True), r=[kkey, qkey], w=[f"ps{sb_}"])
                S.add("pe", lambda e: e.matmul(ps[sb2_][:, 0:n], lhsT=kt1, rhs=qap1, start=True, stop=True), r=[kkey, qkey], w=[f"ps{sb2_}"])
                E = Eb[i][:].rearrange("p a b -> p (a b)")[:, 0:2 * n]
                S.add("act", lambda e: e.activation(out=E[:, 0:n], in_=ps[sb_][:, 0:n], func=AF.Exp, scale=0.125, bias=bias_ap), r=[f"ps{sb_}", "masks"], w=[f"Eb{i}a"])
                S.add("act", lambda e: e.activation(out=E[:, n:2 * n], in_=ps[sb2_][:, 0:n], func=AF.Exp, scale=0.125, bias=bias_ap), r=[f"ps{sb2_}", "masks"], w=[f"Eb{i}b"])
                if mask_ap is not None:
                    S.add("dve", lambda e: e.tensor_tensor(out=E.rearrange("p (a b) -> p a b", a=2), in0=E.rearrange("p (a b) -> p a b", a=2),
                                                            in1=mask_ap.unsqueeze(1).broadcast_to([128, 2, n]), op=ALU.mult), r=[mkey], w=[f"Eb{i}a", f"Eb{i}b"])

            def phase_av():
                i = st_["i"]
                E = Eb[i][:].rearrange("p a b -> p (a b)")[:, 0:2 * n]
                S.add("pe", lambda e: e.matmul(ps[Ob][:, ocol:ocol + 2 * n], lhsT=vap, rhs=E, start=first, stop=last), r=[vkey, f"Eb{i}a", f"Eb{i}b"], w=[f"ps{Ob}"])
                S.add("pe", lambda e: e.matmul(ps[Zb][:, ocol:ocol + 2 * n], lhsT=onesb[:], rhs=E, start=first, stop=last), r=["onesb", f"Eb{i}a", f"Eb{i}b"], w=[f"ps{Zb}"])
            return phase_s, phase_av

        def run_pipelined(blocks):
            if not blocks:
                return
            if blocks[0][0]:
                blocks[0][0]()
            blocks[0][1]()
            for j in range(len(blocks)):
                if j + 1 < len(blocks):
                    if blocks[j + 1][0]:
                        blocks[j + 1][0]()
                    blocks[j + 1][1]()
                if blocks[j][2]:
                    blocks[j][2]()
                blocks[j][3]()

        def attn_epilogue(Ob, Zb, ocol, n, dst, dkey):
            S.add("dve", lambda e: e.tensor_scalar_add(out=T[0][:, 0:2 * n], in0=ps[Zb][:, ocol:ocol + 2 * n], scalar1=1e-30), r=[f"ps{Zb}"], w=["T0"])
            S.add("dve", lambda e: e.reciprocal(out=T[0][:, 0:2 * n], in_=T[0][:, 0:2 * n]), r=["T0"], w=["T0"])
            S.add("dve", lambda e: e.tensor_tensor(out=T[0][:, 0:2 * n], in0=T[0][:, 0:2 * n], in1=ps[Ob][:, ocol:ocol + 2 * n], op=ALU.mult), r=["T0", f"ps{Ob}"], w=["T0"])
            S.add("dve", lambda e: e.scalar_tensor_tensor(out=T[1][:, 0:n], in0=T[0][:, n:2 * n], scalar=neglam, in1=T[0][:, 0:n], op0=ALU.mult, op1=ALU.add), r=["T0", "small"], w=["T1"])
            S.add("dve", lambda e: e.tensor_tensor(out=T[2][:, 0:n], in0=T[1][:, 0:n], in1=T[1][:, 0:n], op=ALU.mult), r=["T1"], w=["T2"])
            b = 6
            S.add("pe", lambda e: e.matmul(ps[b][:, 0:n], lhsT=onesf[:], rhs=T[2][:, 0:n], start=True, stop=True), r=["onesf", "T2"], w=[f"ps{b}"])
            S.add("act", lambda e: e.activation(out=T[2][:, 0:n], in_=ps[b][:, 0:n], func=AF.Sqrt, bias=eps[:, 0:1], scale=1.0 / 128), r=[f"ps{b}", "eps"], w=["T2"])
            S.add("dve", lambda e: e.reciprocal(out=T[2][:, 0:n], in_=T[2][:, 0:n]), r=["T2"], w=["T2"])
            S.add("dve", lambda e: e.scalar_tensor_tensor(out=dst, in0=T[1][:, 0:n], scalar=gsub08, in1=T[2][:, 0:n], op0=ALU.mult, op1=ALU.mult), r=["T1", "T2", "small"], w=[dkey])

        def mem_attn(mkt, mkkey, mv, mvkey, qT, qkey, c0, n, dstB, dkey):
            for m in range(4):
                mem_attn_head(m, mkt, mkkey, mv, mvkey, qT, qkey, c0, n, dstB, dkey)

        def mem_attn_head(m, mkt, mkkey, mv, mvkey, qT, qkey, c0, n, dstB, dkey):
            sb_ = nb()
            for j in range(2):
                for dc in range(2):
                    S.add("pe", lambda e, j=j, dc=dc: e.matmul(ps[sb_][:, j * n:(j + 1) * n], lhsT=mkt[:, 2 * m + dc, j * 128:(j + 1) * 128], rhs=qT[:, 2 * m + dc, c0:c0 + n], start=(dc == 0), stop=(dc == 1)),
                          r=[mkkey, qkey], w=[f"ps{sb_}"])
            i = eslot[0] % 2
            eslot[0] += 1
            E = Eb[i][:].rearrange("p a b -> p (a b)")[:, 0:2 * n]
            S.add("act", lambda e: e.activation(out=E, in_=ps[sb_][:, 0:2 * n], func=AF.Exp, scale=1.0 / 16), r=[f"ps{sb_}"], w=[f"Eb{i}"])
            zb = nb()
            for j in range(2):
                S.add("pe", lambda e, j=j: e.matmul(ps[zb][:, 0:n], lhsT=onesb[:], rhs=E[:, j * n:(j + 1) * n], start=(j == 0), stop=(j == 1)), r=["onesb", f"Eb{i}"], w=[f"ps{zb}"])
            ob = nb()
            for ec in range(2):
                for j in range(2):
                    S.add("pe", lambda e, j=j, ec=ec: e.matmul(ps[ob][:, ec * n:(ec + 1) * n], lhsT=mv[:, j, (2 * m + ec) * 128:(2 * m + ec + 1) * 128], rhs=E[:, j * n:(j + 1) * n], start=(j == 0), stop=(j == 1)),
                          r=[mvkey, f"Eb{i}"], w=[f"ps{ob}"])
            S.add("dve", lambda e: e.reciprocal(out=T[3][:, 0:n], in_=ps[zb][:, 0:n]), r=[f"ps{zb}"], w=["T3"])
            for ec in range(2):
                S.add("dve", lambda e, ec=ec: e.tensor_tensor(out=dstB[:, 2 * m + ec, c0:c0 + n], in0=ps[ob][:, ec * n:(ec + 1) * n], in1=T[3][:, 0:n], op=ALU.mult),
                      r=[f"ps{ob}", "T3"], w=[dkey])

        for t in range(2):
            front(memx[t * 128:(t + 1) * 128, :], QT, "QT", t * 128)
        chk(41)
        for f0 in range(0, 1024, 256):
            wbuf, wkey = wslab(w_mkv, 0, 8, f0, 256, gain=9)
            for fi in range(2):
                b = mm_fm(wbuf, wkey, 8, fi, QT, "QT", 0, 256)
                S.add("act", lambda e, b=b, fc=f0 // 128 + fi: e.activation(out=MKT[:, fc, :], in_=ps[b][:, 0:256], func=AF.Copy), r=[f"ps{b}"], w=["MKT"])
            for t in range(2):
                b = nb()
                mm_tm(QT, "QT", t * 128, wbuf, wkey, 8, 256, b)
                S.add("act", lambda e, b=b: e.activation(out=stg[:], in_=ps[b][:, 0:256], func=AF.Copy), r=[f"ps{b}"], w=["stg"])
                S.dma("sp", mem_k[t * 128:(t + 1) * 128, f0:f0 + 256], stg[:], r=["stg"], w=["D:mem_k"])
        chk(42)
        for f0 in range(0, 1024, 256):
            wbuf, wkey = wslab(w_mkv, 0, 8, 1024 + f0, 256, gain=9)
            for t in range(2):
                b = nb()
                mm_tm(QT, "QT", t * 128, wbuf, wkey, 8, 256, b)
                S.add("act", lambda e, b=b: e.activation(out=stg[:], in_=ps[b][:, 0:256], func=AF.Copy), r=[f"ps{b}"], w=["stg"])
                S.add("dve", lambda e, b=b, t=t, f0=f0: e.tensor_copy(out=MV[:, t, f0:f0 + 256], in_=ps[b][:, 0:256]), r=[f"ps{b}"], w=["MV"])
                S.dma("sp", mem_v[t * 128:(t + 1) * 128, f0:f0 + 256], stg[:], r=["stg"], w=["D:mem_v"])
        OUTKEYS += ["D:mem_k", "D:mem_v"]
        chk(5)

        def lru_chunk(c, zx_bank_list, npc, has_s, t0, full, zy_src):
            ntot = npc + (128 if has_s else 0)
            S.add("dve", lambda e: e.tensor_copy(out=EXTL[:, 0:3], in_=halo[:, c, :]), r=["halo"], w=["EXTL"])
            for (b, c0, n) in zx_bank_list:
                pc = min(n, max(0, npc - c0))
                if pc > 0:
                    S.add("act", lambda e, b=b, c0=c0, pc=pc: e.activation(out=EXTL[:, 3 + c0:3 + c0 + pc], in_=ps[b][:, 0:pc], func=AF.Copy), r=[f"ps{b}"], w=["EXTL"])
                if pc < n:
                    S.add("act", lambda e, b=b, pc=pc: e.activation(out=EXTS[:, :, 3:11], in_=ps[b][:, pc:pc + 128].rearrange("p (s t) -> p s t", t=8), func=AF.Copy), r=[f"ps{b}"], w=["EXTS"])
            if has_s:
                S.add("dve", lambda e: e.tensor_copy(out=EXTS[:, :, 0:3], in_=stT_c[:, c, :].rearrange("p (s j) -> p s j", j=3)), r=["stT_c"], w=["EXTS"])
            S.add("dve", lambda e: e.tensor_copy(out=halo[:, c, :], in_=EXTL[:, npc:npc + 3]), r=["EXTL"], w=["halo"])
            xc = T[0]
            def conv(dst, src_fn):
                S.add("dve", lambda e: e.tensor_scalar(out=dst, in0=src_fn(0), scalar1=vec8[:, 1, c:c + 1], scalar2=vec8[:, 5, c:c + 1], op0=ALU.mult, op1=ALU.add), r=["EXTL", "EXTS", "vec8"], w=["T0"])
                for j in range(1, 4):
                    S.add("dve", lambda e, j=j: e.scalar_tensor_tensor(out=dst, in0=src_fn(j), scalar=vec8[:, 1 + j, c:c + 1], in1=dst, op0=ALU.mult, op1=ALU.add), r=["EXTL", "EXTS", "vec8", "T0"], w=["T0"])
            conv(xc[:, 0:npc], lambda j: EXTL[:, j:j + npc])
            if has_s:
                conv(xc[:, npc:ntot].rearrange("p (s t) -> p s t", t=8), lambda j: EXTS[:, :, j:j + 8])
            S.add("act", lambda e: e.activation(out=ktb[:, 0:ntot], in_=xc[:, 0:ntot], func=AF.Copy), r=["T0"], w=["ktb"])
            for (wg, wgk, bi, tdst) in ((wa_b, "wa_b", 6, 1), (wi_b, "wi_b", 7, 2)):
                b = nb()
                S.add("pe", lambda e, b=b, wg=wg: e.matmul(ps[b][:, 0:ntot], lhsT=wg[:, c, :], rhs=ktb[:, 0:ntot], start=True, stop=True), r=[wgk, "ktb"], w=[f"ps{b}"])
                S.add("act", lambda e, b=b, bi=bi, tdst=tdst: e.activation(out=T[tdst][:, 0:ntot], in_=ps[b][:, 0:ntot], func=AF.Sigmoid, bias=vec8[:, bi, c:c + 1]), r=[f"ps{b}", "vec8"], w=[f"T{tdst}"])
            S.add("act", lambda e: e.activation(out=T[3][:, 0:ntot], in_=T[1][:, 0:ntot], func=AF.Exp, scale=coef[:, 0, c:c + 1]), r=["T1", "coef"], w=["T3"])
            S.add("act", lambda e: e.activation(out=T[4][:, 0:ntot], in_=T[1][:, 0:ntot], func=AF.Exp, scale=coef[:, 1, c:c + 1]), r=["T1", "coef"], w=["T4"])
            S.add("dve", lambda e: e.tensor_scalar(out=T[4][:, 0:ntot], in0=T[4][:, 0:ntot], scalar1=-1.0, scalar2=1.0, op0=ALU.mult, op1=ALU.add), r=["T4"], w=["T4"])
            S.add("act", lambda e: e.activation(out=T[4][:, 0:ntot], in_=T[4][:, 0:ntot], func=AF.Sqrt), r=["T4"], w=["T4"])
            S.add("dve", lambda e: e.tensor_tensor(out=T[4][:, 0:ntot], in0=T[4][:, 0:ntot], in1=T[2][:, 0:ntot], op=ALU.mult), r=["T4", "T2"], w=["T4"])
            S.add("dve", lambda e: e.tensor_tensor(out=T[4][:, 0:ntot], in0=T[4][:, 0:ntot], in1=xc[:, 0:ntot], op=ALU.mult), r=["T4", "T0"], w=["T4"])
            ntile = npc // 128
            S.add("dve", lambda e: e.tensor_tensor(out=T[4][:, 0:npc].rearrange("p (a b) -> p a b", b=128), in0=T[4][:, 0:npc].rearrange("p (a b) -> p a b", b=128),
                                                    in1=masks[:, 1, t0:t0 + ntile].unsqueeze(2).broadcast_to([128, ntile, 128]), op=ALU.mult), r=["T4", "masks"], w=["T4"])
            S.add("dve", lambda e: e.tensor_tensor_scan(out=T[5][:, 0:npc], data0=T[3][:, 0:npc], data1=T[4][:, 0:npc], initial=hstate[:, c:c + 1], op0=ALU.mult, op1=ALU.add), r=["T3", "T4", "hstate"], w=["T5"])
            S.add("dve", lambda e: e.tensor_copy(out=hstate[:, c:c + 1], in_=T[5][:, npc - 1:npc]), r=["T5"], w=["hstate"])
            if has_s:
                for s in range(16):
                    S.add("dve", lambda e, s=s: e.tensor_tensor_scan(out=T[5][:, npc + s * 8:npc + s * 8 + 8], data0=T[3][:, npc + s * 8:npc + s * 8 + 8], data1=T[4][:, npc + s * 8:npc + s * 8 + 8],
                                                                      initial=stT_h[:, c, s:s + 1], op0=ALU.mult, op1=ALU.add), r=["T3", "T4", "stT_h"], w=["T5"])
                S.add("dve", lambda e: e.tensor_copy(out=hsS[:, c, :], in_=T[5][:, npc:ntot].rearrange("p (s t) -> p s t", t=8)[:, :, 7]), r=["T5"], w=["hsS"])
            if full:
                zb, zk = zy_src
                gelu_mul(BR[0][:, c, 0:ntot], "BR0", zb, zk, T[5][:, 0:ntot], "T5", ntot, 6, 7)

        def run_sg(t0, t1, has_s, full):
            ntile = t1 - t0
            npc = ntile * 128
            ntot = npc + (128 if has_s else 0)
            tiles = [(t, t * 128, (t - t0) * 128) for t in range(t0, t1)]
            if has_s:
                tiles.append(("S", 8192, npc))
            for (t, row, col) in tiles:
                front(xs[row:row + 128, :], hT, "hT", col)
            for f0 in range(0, 1024, 256):
                wbuf, wkey = wslab(w_in, 0, 8, 3072 + f0, 256, gain=0)
                for fi in range(2):
                    h = f0 // 128 + fi
                    b = mm_fm(wbuf, wkey, 8, fi, hT, "hT", 0, ntot)
                    S.add("act", lambda e, b=b: e.activation(out=ktb[:, 0:ntot], in_=ps[b][:, 0:ntot], func=AF.Copy), r=[f"ps{b}"], w=["ktb"])
                    S.dma("sp", KTs[h, :, t0 * 128:t0 * 128 + npc], ktb[:, 0:npc], r=["ktb"], w=[f"D:KTs{h}"])
                    if has_s:
                        S.add("dve", lambda e, h=h: e.tensor_copy(out=KTsm[:, h, :], in_=ktb[:, npc:ntot]), r=["ktb"], w=["KTsm"])
                if full:
                    for (t, row, col) in tiles:
                        if t == 47:
                            continue
                        b = nb()
                        mm_tm(hT, "hT", col, wbuf, wkey, 8, 256, b)
                        S.add("act", lambda e, b=b: e.activation(out=stg[:], in_=ps[b][:, 0:256], func=AF.Copy), r=[f"ps{b}"], w=["stg"])
                        dst = k_s[:, f0:f0 + 256] if t == "S" else k_own[(t - 48) * 128:(t - 47) * 128, f0:f0 + 256]
                        S.dma("sp", dst, stg[:], r=["stg"], w=["D:k"])
            for f0 in range(0, 1024, 256):
                wbuf, wkey = wslab(w_in, 0, 8, 4096 + f0, 256, gain=0)
                for (t, row, col) in tiles:
                    b = nb()
                    mm_tm(hT, "hT", col, wbuf, wkey, 8, 256, b)
                    if t == "S":
                        S.add("dve", lambda e, b=b, f0=f0: e.tensor_copy(out=vbs[:, f0:f0 + 256], in_=ps[b][:, 0:256]), r=[f"ps{b}"], w=["vbs"])
                    else:
                        S.add("dve", lambda e, b=b: e.tensor_copy(out=hb[0][:, 0:256], in_=ps[b][:, 0:256]), r=[f"ps{b}"], w=["hb0"])
                        for fi in range(2):
                            S.dma("sp", Vs[f0 // 128 + fi, t * 128:(t + 1) * 128, :], hb[0][:, fi * 128:(fi + 1) * 128], r=["hb0"], w=[f"D:Vs{f0 // 128 + fi}"])
                    if full and t != 47:
                        S.add("act", lambda e, b=b: e.activation(out=stg[:], in_=ps[b][:, 0:256], func=AF.Copy), r=[f"ps{b}"], w=["stg"])
                        dst = v_s[:, f0:f0 + 256] if t == "S" else v_own[(t - 48) * 128:(t - 47) * 128, f0:f0 + 256]
                        S.dma("sp", dst, stg[:], r=["stg"], w=["D:v"])
            for f0 in range(0, 1024, 256):
                wbx, wkx = wslab(w_in, 0, 8, f0, 256, gain=0)
                if full:
                    wby, wky = wslab(w_in, 0, 8, 1024 + f0, 256, gain=0)
                for fi in range(2):
                    c = f0 // 128 + fi
                    b = mm_fm(wbx, wkx, 8, fi, hT, "hT", 0, ntot)
                    zy = None
                    if full:
                        b2 = mm_fm(wby, wky, 8, fi, hT, "hT", 0, ntot)
                        S.add("act", lambda e, b2=b2: e.activation(out=T[7][:, 0:ntot], in_=ps[b2][:, 0:ntot], func=AF.Copy), r=[f"ps{b2}"], w=["T7"])
                        zy = (T[7][:, 0:ntot], "T7")
                    lru_chunk(c, [(b, 0, ntot)], npc, has_s, t0, full, zy)
                if full and t1 == 64:
                    for (t, row, col) in tiles[-2:]:
                        b = nb()
                        mm_tm(hT, "hT", col, wbx, wkx, 8, 256, b)
                        S.add("act", lambda e, b=b: e.activation(out=stg[:], in_=ps[b][:, 0:256], func=AF.Copy), r=[f"ps{b}"], w=["stg"])
                        if t == "S":
                            for j in range(3):
                                S.dma("sp", lru_conv_s[:, j, f0:f0 + 256], stg[5 + j::8, :], r=["stg"], w=["D:lcs"])
                        else:
                            S.dma("sp", lru_conv_p[:, f0:f0 + 256], stg[125:128, :], r=["stg"], w=["D:lcp"])
            if not full:
                return
            chk(10)
            for f0 in range(0, 1024, 256):
                wbuf, wkey = wslab(w_in, 0, 8, 2048 + f0, 256, gain=0)
                for fi in range(2):
                    b = mm_fm(wbuf, wkey, 8, fi, hT, "hT", 0, ntot)
                    S.add("act", lambda e, b=b, h=f0 // 128 + fi: e.activation(out=QT[:, h, 0:ntot], in_=ps[b][:, 0:ntot], func=AF.Copy), r=[f"ps{b}"], w=["QT"])
            chk(11)
            for h in range(8):
                nk = t1 * 128
                S.dma("sp", KTh[:, 0:nk], KTs[h, :, 0:nk], r=[f"D:KTs{h}"], w=["KTh"])
                S.dma("sp", Vh[:, 0:t1, :], Vs[h, 0:nk, :].rearrange("(t p) e -> p t e", p=128), r=[f"D:Vs{h}"], w=["Vh"])
                for qi, (t, row, col) in enumerate(tiles):
                    if t == "S":
                        continue
                    Ob, Zb = 4, 5
                    blocks = []
                    for kt in range(t + 1):
                        ps_, pav_ = attn_block(QT[0:64, h, col:col + 128], QT[64:128, h, col:col + 128], "QT",
                                               KTh[0:64, kt * 128:(kt + 1) * 128], KTh[64:128, kt * 128:(kt + 1) * 128], "KTh",
                                               Vh[:, kt, :], "Vh", 128, masks[:, 0, kt:kt + 1], trib[:] if kt == t else None, "trib",
                                               Ob, Zb, 0, kt == 0, kt == t, kt % 2)
                        blocks.append((None, ps_, None, pav_))
                    run_pipelined(blocks)
                    attn_epilogue(Ob, Zb, 0, 128, BR[1][:, h, col:col + 128], "BR1")
            chk(12)
            if has_s:
                for s in range(16):
                    for zbk in (4, 5):
                        S.add("pe", lambda e, zbk=zbk: e.matmul(ps[zbk][:, 0:128], lhsT=zerob[:], rhs=onesb[:], start=True, stop=False), r=["zerob", "onesb"], w=[f"ps{zbk}"])
                    blocks = []
                    for pg in range(17):
                        pre_s = pre_av = None
                        if pg < 16:
                            def pre_s(s=s, pg=pg):
                                S.idma(xt[0][:], ck, pidx[:, s * 16 + pg:s * 16 + pg + 1], r=["pidx"], w=["xt0"])
                                S.add("act", lambda e: e.activation(out=hb[0][:], in_=xt[0][:], func=AF.Copy), r=["xt0"], w=["hb0"])
                                for k in range(8):
                                    S.add("pe", lambda e, k=k: e.transpose(out=pstb[:, k * 128:(k + 1) * 128], in_=hb[0][:, k * 128:(k + 1) * 128], identity=identb[:]), r=["hb0", "identb"], w=["ps7"])
                                S.add("dve", lambda e: e.tensor_copy(out=KTp[:].rearrange("p a b -> p (a b)"), in_=pstb), r=["ps7"], w=["KTp"])

                            def pre_av(s=s, pg=pg):
                                S.idma(xt[1][:], cv, pidx[:, s * 16 + pg:s * 16 + pg + 1], r=["pidx"], w=["xt1"])
                                S.add("dve", lambda e: e.tensor_copy(out=hb[1][:], in_=xt[1][:]), r=["xt1"], w=["hb1"])
                            ktsrc, ktkey, vsrc, vkey, msk = KTp, "KTp", hb[1], "hb1", None
                        else:
                            ktsrc, ktkey, vsrc, vkey, msk = KTsm, "KTsm", vbs, "vbs", smaskb[:, s * 8:s * 8 + 8]
                        for h in range(8):
                            ps_, pav_ = attn_block(QT[0:64, h, npc + s * 8:npc + s * 8 + 8], QT[64:128, h, npc + s * 8:npc + s * 8 + 8], "QT",
                                                   ktsrc[0:64, h, :], ktsrc[64:128, h, :], ktkey, vsrc[:, h * 128:(h + 1) * 128], vkey, 8,
                                                   0.0, msk, "smaskb", 4, 5, h * 16, False, pg == 16, h % 2)
                            blocks.append((pre_s if h == 0 else None, ps_, pre_av if h == 0 else None, pav_))
                    run_pipelined(blocks)
                    for h in range(8):
                        attn_epilogue(4, 5, h * 16, 8, BR[1][:, h, npc + s * 8:npc + s * 8 + 8], "BR1")
            chk(13)
            for f0 in range(0, 1024, 256):
                wbuf, wkey = wslab(w_in, 0, 8, 5120 + f0, 256, gain=0)
                for fi in range(2):
                    b = mm_fm(wbuf, wkey, 8, fi, hT, "hT", 0, ntot)
                    S.add("act", lambda e, b=b, fc=f0 // 128 + fi: e.activation(out=QT[:, fc, 0:ntot], in_=ps[b][:, 0:ntot], func=AF.Copy), r=[f"ps{b}"], w=["QT"])
            for (t, row, col) in tiles:
                if t == "S":
                    for s in range(16):
                        for (src, dst3, key, tr) in ((cmk, MKTs, "MKTs", True), (cmv, MVs, "MVs", False)):
                            for j in range(2):
                                S.dma("sp", xt[j][:], src[s * 256 + j * 128:s * 256 + (j + 1) * 128, :], w=[f"xt{j}"])
                                if tr:
                                    S.add("act", lambda e, j=j: e.activation(out=hb[j][:], in_=xt[j][:], func=AF.Copy), r=[f"xt{j}"], w=[f"hb{j}"])
                                    for k in range(8):
                                        S.add("pe", lambda e, k=k, j=j: e.transpose(out=pstb[:, k * 128:(k + 1) * 128], in_=hb[j][:, k * 128:(k + 1) * 128], identity=identb[:]), r=[f"hb{j}", "identb"], w=["ps7"])
                                    S.add("dve", lambda e, j=j: e.tensor_copy(out=MKTs[:, :, j * 128:(j + 1) * 128], in_=pstb.rearrange("p (a b) -> p a b", a=8)), r=["ps7"], w=["MKTs"])
                                else:
                                    S.add("dve", lambda e, j=j: e.tensor_copy(out=MVs[:, j, :], in_=xt[j][:]), r=[f"xt{j}"], w=["MVs"])
                        mem_attn(MKTs, "MKTs", MVs, "MVs", QT, "QT", col + s * 8, 8, BR[2], "BR2")
                else:
                    mem_attn(MKT, "MKT", MV, "MV", QT, "QT", col, 128, BR[2], "BR2")
            chk(14)
            for fc in range(8):
                for bidx in range(3):
                    wg, wgk = wslab(w_in, 0, 8, 6144 + bidx * 1024 + fc * 128, 128, gain=0)
                    b = mm_fm(wg, wgk, 8, 0, hT, "hT", 0, ntot)
                    S.add("act", lambda e, b=b: e.activation(out=T[1][:, 0:ntot], in_=ps[b][:, 0:ntot], func=AF.Sigmoid), r=[f"ps{b}"], w=["T1"])
                    wp, wpk = wslab(w_br[bidx], 0, 8, fc * 128, 128)
                    b2 = mm_fm(wp, wpk, 8, 0, BR[bidx], f"BR{bidx}", 0, ntot)
                    if bidx == 0:
                        S.add("dve", lambda e, b2=b2: e.tensor_tensor(out=T[2][:, 0:ntot], in0=ps[b2][:, 0:ntot], in1=T[1][:, 0:ntot], op=ALU.mult), r=[f"ps{b2}", "T1"], w=["T2"])
                    else:
                        S.add("dve", lambda e, b2=b2: e.tensor_tensor(out=T[3][:, 0:ntot], in0=ps[b2][:, 0:ntot], in1=T[1][:, 0:ntot], op=ALU.mult), r=[f"ps{b2}", "T1"], w=["T3"])
                        S.add("dve", lambda e: e.tensor_tensor(out=T[2][:, 0:ntot], in0=T[2][:, 0:ntot], in1=T[3][:, 0:ntot], op=ALU.add), r=["T2", "T3"], w=["T2"])
                S.add("act", lambda e, fc=fc: e.activation(out=QT[:, fc, 0:ntot], in_=T[2][:, 0:ntot], func=AF.Copy), r=["T2"], w=["QT"])
            chk(15)
            for f0 in range(0, 1024, 256):
                wbuf, wkey = wslab(w_out, 0, 8, f0, 256)
                for li, (t, row, col) in enumerate(tiles):
                    b = nb()
                    mm_tm(QT, "QT", col, wbuf, wkey, 8, 256, b)
                    S.add("act", lambda e, b=b, li=li, f0=f0: e.activation(out=YB[:, li, f0:f0 + 256], in_=ps[b][:, 0:256], func=AF.Copy), r=[f"ps{b}"], w=[f"YB{li}"])

            def post_norm_res(li, gi, resid_ap, rkey, out_ap, okey):
                S.add("act", lambda e: e.activation(out=junk[:], in_=YB[:, li, :], func=AF.Square, scale=1.0 / 32, accum_out=ssb[:, 2:3]), r=[f"YB{li}"], w=["junk", "ss2"])
                S.add("act", lambda e: e.activation(out=ssb[:, 2:3], in_=ssb[:, 2:3], func=AF.Sqrt, bias=eps[:, 0:1], scale=1.0), r=["ss2", "eps"], w=["ss2"])
                S.add("dve", lambda e: e.reciprocal(out=ssb[:, 2:3], in_=ssb[:, 2:3]), r=["ss2"], w=["ss2"])
                S.add("dve", lambda e: e.scalar_tensor_tensor(out=YB[:, li, :], in0=YB[:, li, :], scalar=ssb[:, 2:3], in1=bcg[:, gi, :], op0=ALU.mult, op1=ALU.mult), r=[f"YB{li}", "ss2", "bcg"], w=[f"YB{li}"])
                S.add("dve", lambda e: e.tensor_tensor(out=out_ap, in0=YB[:, li, :], in1=resid_ap, op=ALU.add), r=[f"YB{li}", rkey], w=[okey])

            for li, (t, row, col) in enumerate(tiles):
                i = xslot[0] % 2
                xslot[0] += 1
                S.dma("sp", xt[i][:], xs[row:row + 128, :], w=[f"xt{i}"])
                post_norm_res(li, 0, xt[i][:], f"xt{i}", X1[:, li, :], f"X1{li}")
                front(None, hT, "hT", col, xkeep=(X1[:, li, :], f"X1{li}"))
            chk(16)
            for c in range(24):
                wgb, wgk = wslab(w_up, 0, 8, c * 128, 128, gain=10)
                wvb, wvk = wslab(w_up, 0, 8, 3072 + c * 128, 128, gain=10)
                for (wbuf_, wkey_, EX, EXS, exk, cc) in ((wgb, wgk, EXTG, EXSG, "EXTG", c), (wvb, wvk, EXTV, EXSV, "EXTV", 24 + c)):
                    b = mm_fm(wbuf_, wkey_, 8, 0, hT, "hT", 0, ntot)
                    S.add("dve", lambda e, EX=EX, cc=cc: e.tensor_copy(out=EX[:, 0:2], in_=FH[:, cc, :]), r=["FH"], w=[exk])
                    S.add("act", lambda e, b=b, EX=EX: e.activation(out=EX[:, 2:2 + npc], in_=ps[b][:, 0:npc], func=AF.Copy), r=[f"ps{b}"], w=[exk])
                    if t0 == 47:
                        S.add("dve", lambda e, EX=EX: e.tensor_scalar_mul(out=EX[:, 0:130], in0=EX[:, 0:130], scalar1=masks[:, 1, 47:48]), r=[exk, "masks"], w=[exk])
                    S.add("dve", lambda e, EX=EX, cc=cc: e.tensor_copy(out=FH[:, cc, :], in_=EX[:, npc:npc + 2]), r=[exk], w=["FH"])
                    if has_s:
                        S.add("act", lambda e, b=b, EXS=EXS: e.activation(out=EXS[:, :, 2:10], in_=ps[b][:, npc:ntot].rearrange("p (s t) -> p s t", t=8), func=AF.Copy), r=[f"ps{b}"], w=[exk])
                        S.add("dve", lambda e, EXS=EXS, cc=cc: e.tensor_copy(out=EXS[:, :, 0:2], in_=stT_f[:, cc, :].rearrange("p (s j) -> p s j", j=2)), r=["stT_f"], w=[exk])
                    tdst = 5 if cc < 24 else 6

                    def conv3(dst, src_fn, cc=cc, exk=exk, tdst=tdst):
                        S.add("dve", lambda e: e.tensor_scalar(out=dst, in0=src_fn(0), scalar1=ffnv[:, 0, cc:cc + 1], scalar2=ffnv[:, 3, cc:cc + 1], op0=ALU.mult, op1=ALU.add), r=[exk, "ffnv"], w=[f"T{tdst}"])
                        for j in range(1, 3):
                            S.add("dve", lambda e, j=j: e.scalar_tensor_tensor(out=dst, in0=src_fn(j), scalar=ffnv[:, j, cc:cc + 1], in1=dst, op0=ALU.mult, op1=ALU.add), r=[exk, "ffnv", f"T{tdst}"], w=[f"T{tdst}"])
                    conv3(T[tdst][:, 0:npc], lambda j, EX=EX: EX[:, j:j + npc])
                    if has_s:
                        conv3(T[tdst][:, npc:ntot].rearrange("p (s t) -> p s t", t=8), lambda j, EXS=EXS: EXS[:, :, j:j + 8])
                    if t1 == 64:
                        for (t, row, col) in tiles[-2:]:
                            b3 = nb()
                            mm_tm(hT, "hT", col, wbuf_, wkey_, 8, 128, b3)
                            S.add("act", lambda e, b3=b3: e.activation(out=stg[:, 0:128], in_=ps[b3][:, 0:128], func=AF.Copy), r=[f"ps{b3}"], w=["stg"])
                            if t == "S":
                                for j in range(2):
                                    S.dma("sp", ffn_conv_s[:, j, cc * 128:(cc + 1) * 128], stg[6 + j::8, 0:128], r=["stg"], w=["D:fcs"])
                            else:
                                S.dma("sp", ffn_conv_p[:, cc * 128:(cc + 1) * 128], stg[126:128, 0:128], r=["stg"], w=["D:fcp"])
                gelu_mul(BR[c // 8][:, c % 8, 0:ntot], f"BR{c // 8}", T[5][:, 0:ntot], "T5", T[6][:, 0:ntot], "T6", ntot, 4, 7)
            chk(17)
            for f0 in range(0, 1024, 256):
                banks = [nb() for _ in tiles]
                for g in range(3):
                    wbuf, wkey = wslab(w_down, g * 8, 8, f0, 256)
                    for li, (t, row, col) in enumerate(tiles):
                        mm_tm(BR[g], f"BR{g}", col, wbuf, wkey, 8, 256, banks[li], start=(g == 0), stop=(g == 2))
                for li, (t, row, col) in enumerate(tiles):
                    b = banks[li]
                    S.add("act", lambda e, b=b, li=li, f0=f0: e.activation(out=YB[:, li, f0:f0 + 256], in_=ps[b][:, 0:256], func=AF.Copy), r=[f"ps{b}"], w=[f"YB{li}"])
            for li, (t, row, col) in enumerate(tiles):
                post_norm_res(li, 1, X1[:, li, :], f"X1{li}", YB[:, li, :], f"YB{li}")
                if t == 47:
                    continue
                dst = y_s[:, :] if t == "S" else y_own[(t - 48) * 128:(t - 47) * 128, :]
                S.dma("sp", dst, YB[:, li, :], r=[f"YB{li}"], w=["D:y"])

        for (a, b_) in light_sgs:
            run_sg(a, b_, False, False)
        for (a, b_) in full_sgs:
            run_sg(a, b_, b_ == 64, True)
        chk(6)
        S.muted = (dbg == 7)
        S.dma("sp", lru_h_p.rearrange("o (c p) -> p (o c)", p=128), hstate[:], r=["hstate"], w=["D:lhp"])
        for c in range(8):
            S.dma("sp", lru_h_s[:, c * 128:(c + 1) * 128].rearrange("s p -> p s"), hsS[:, c, :], r=["hsS"], w=["D:lhs"])
        OUTKEYS += ["D:k", "D:v", "D:lcs", "D:lcp", "D:fcs", "D:fcp", "D:y", "D:lhp", "D:lhs"]
        S.muted = False
        S.add("sp", None, r=OUTKEYS)
        S.emit()
    return nc


def _lay8(v):
    return np.ascontiguousarray(np.asarray(v, np.float32).reshape(-1, 128).T)


def make_in_maps(inp, with_cache=True):
    f = lambda k: np.asarray(inp[k])
    n = 8
    vec8 = np.stack([_lay8(f("g_pre_mix")[0])] + [_lay8(f("w_lru_conv")[0, j]) for j in range(4)] +
                    [_lay8(f(k)[0]) for k in ("b_lru_conv", "b_rg_a", "b_rg_i", "lru_lambda", "g_mem", "g_pre_ffn")], axis=1)
    ffnv = np.stack([_lay8(f("w_ffn_conv")[0, j]) for j in range(3)] + [_lay8(f("b_ffn_conv")[0])], axis=1)
    bc = np.ascontiguousarray(np.broadcast_to(np.stack([f("g_post_mix")[0], f("g_post_ffn")[0]])[None], (128, 2, 1024)))
    lamv = np.ascontiguousarray(np.broadcast_to(np.stack([f("lambda_q1")[0], f("lambda_k1")[0], f("lambda_q2")[0], f("lambda_k2")[0]])[None], (128, 4, 64)))
    gsub = np.ascontiguousarray(f("g_subln")[0].reshape(128, 1))
    ident = np.eye(128, dtype=np.float32)
    tri = np.triu(np.ones((128, 128), np.float32))
    pp = np.arange(128)
    smask = ((pp[:, None] // 8 == pp[None, :] // 8) & (pp[:, None] % 8 <= pp[None, :] % 8)).astype(np.float32)
    consts = np.ascontiguousarray(np.stack([ident, tri, smask], axis=1))
    iota = np.arange(128, dtype=np.float32).reshape(128, 1)
    shared = {
        "w_in": f("w_in")[0], "w_rga": f("w_rg_a")[0].reshape(1024, 128), "w_rgi": f("w_rg_i")[0].reshape(1024, 128),
        "w_mkv": f("w_mem_kv")[0], "w_br0": f("w_br_lru")[0], "w_br1": f("w_br_attn")[0], "w_br2": f("w_br_mem")[0],
        "w_out": f("w_out")[0], "w_up": f("w_up")[0], "w_down": f("w_down")[0],
        "vec8": vec8, "ffnv": ffnv, "bc": bc, "lamv": lamv, "gsub": gsub, "consts": consts, "iota": iota,
    }
    if with_cache:
        shared["ck"] = f("cache_k")[0].reshape(-1, 1024)
        shared["cv"] = f("cache_v")[0].reshape(-1, 1024)
    xp = f("x_prompt"); xsmp = f("x_sample").reshape(1024, 1024)
    in_maps = []
    for c in range(n):
        b, j = divmod(c, 4)
        xs = np.zeros((8320, 1024), np.float32)
        nreal = (j + 1) * 2048
        xs[8192 - nreal:8192] = xp[b, :nreal]
        xs[8192:] = xsmp[c * 128:(c + 1) * 128]
        npad_t = (8192 - nreal) // 128
        masks = np.zeros((128, 2, 64), np.float32)
        masks[:, 0, :npad_t] = -30000.0
        masks[:, 1, npad_t:] = 1.0
        sl = slice(c * 16, (c + 1) * 16)
        m = dict(shared)
        m.update({
            "xs": xs, "memx": f("mem_prompt")[b], "pt": f("page_table")[sl].reshape(1, 256).astype(np.int32),
            "cmk": f("cache_mem_k")[0, sl].reshape(4096, 1024), "cmv": f("cache_mem_v")[0, sl].reshape(4096, 1024),
            "st_h": f("state_lru_h")[0, sl], "st_c": f("state_lru_conv")[0, sl].reshape(48, 1024),
            "st_f": f("state_ffn_conv")[0, sl].reshape(32, 6144), "masks": masks,
        })
        in_maps.append(m)
    return in_maps


def launch(in_maps, **bkw):
    nc = build(**bkw)
    return run_bass_kernel_spmd(nc, in_maps, core_ids=list(range(len(in_maps)))).results


def assemble(res):
    cat = lambda k, cs: np.concatenate([res[c][k] for c in cs], axis=0)
    y_p = np.stack([cat("y_own", range(4 * b, 4 * b + 4)) for b in range(2)])
    k_p = np.stack([cat("k_own", range(4 * b, 4 * b + 4)) for b in range(2)]).reshape(1, 2, 8192, 8, 2, 64)
    v_p = np.stack([cat("v_own", range(4 * b, 4 * b + 4)) for b in range(2)]).reshape(1, 2, 8192, 8, 128)
    outs = (
        y_p, cat("y_s", range(8)).reshape(128, 8, 1024), k_p, v_p,
        np.stack([res[3 + 4 * b]["mem_k"] for b in range(2)]).reshape(1, 2, 256, 4, 256),
        np.stack([res[3 + 4 * b]["mem_v"] for b in range(2)]).reshape(1, 2, 256, 4, 256),
        np.stack([res[3 + 4 * b]["lru_h_p"][0] for b in range(2)]).reshape(1, 2, 1024),
        np.stack([res[3 + 4 * b]["lru_conv_p"] for b in range(2)]).reshape(1, 2, 3, 1024),
        np.stack([res[3 + 4 * b]["ffn_conv_p"] for b in range(2)]).reshape(1, 2, 2, 6144),
        cat("k_s", range(8)).reshape(1, 128, 8, 8, 2, 64), cat("v_s", range(8)).reshape(1, 128, 8, 8, 128),
        cat("lru_h_s", range(8)).reshape(1, 128, 1024), cat("lru_conv_s", range(8)).reshape(1, 128, 3, 1024),
        cat("ffn_conv_s", range(8)).reshape(1, 128, 2, 6144),
    )
    return tuple(np.ascontiguousarray(o, dtype=np.float32) for o in outs)


def kernel(**inp):
    return assemble(launch(make_in_maps(inp)))
```
